# Optimizing a Trainium2 kernel written in Bass

```python
import jax, jax.numpy as jnp
from jax import lax
import numpy as np

D_MODEL = 2048
BATCH = 2
SEQ = 4096
DEPTH = 2
DEC_BATCH = 128
DEC_SEQ = 4
PAST_LEN = 8192
PAGE_SIZE = 128

N_META = 16
POOL_WIDTH = D_MODEL // 2
POOL_WINDOWS = (2, 4, 8, 16)
POOL_GROUPS = len(POOL_WINDOWS)
POOL_GROUP_DIM = POOL_WIDTH // POOL_GROUPS
POOL_STATE = max(POOL_WINDOWS) - 1
N_HEADS = D_MODEL // 256
NOPE_DIM = 128
ROPE_DIM = 64
ROPE_HALF = ROPE_DIM // 2
QK_DIM = NOPE_DIM + ROPE_DIM
V_DIM = 128
ATTN_WIDTH = N_HEADS * V_DIM
MIX_WIDTH = POOL_WIDTH + ATTN_WIDTH
Q_LORA = D_MODEL // 4
KV_LORA = D_MODEL // 8
IN_WIDTH = POOL_WIDTH + Q_LORA + KV_LORA + ROPE_DIM
D_FF = 5632
ROPE_THETA = 10000.0
EPS = 1e-6
Q_BLOCK = 128
ATTN_SCALE = QK_DIM ** -0.5
NEG_INF = -1e30
F32 = jnp.float32

kernel_name = "hymba_pool_mla_macaron_step"


def rmsnorm(x, g):
    xf = x.astype(F32)
    xf = xf * lax.rsqrt(jnp.mean(xf * xf, axis=-1, keepdims=True) + EPS)
    return (xf * g.astype(F32)).astype(x.dtype)


def swiglu_half(x, g, w_gate, w_up, w_down):
    h = rmsnorm(x, g)
    a = jax.nn.silu(h @ w_gate) * (h @ w_up)
    return x + 0.5 * (a @ w_down)


def rope_tables(pos):
    inv = ROPE_THETA ** (-jnp.arange(ROPE_HALF, dtype=F32) / ROPE_HALF)
    ang = pos.astype(F32)[:, None] * inv[None, :]
    return jnp.cos(ang), jnp.sin(ang)


def apply_rope(x, cos, sin):
    shape = (x.shape[1],) + (1,) * (x.ndim - 3) + (ROPE_HALF,)
    c, s = cos.reshape(shape), sin.reshape(shape)
    x1 = x[..., :ROPE_HALF].astype(F32)
    x2 = x[..., ROPE_HALF:].astype(F32)
    return jnp.concatenate([x1 * c - x2 * s, x1 * s + x2 * c], -1).astype(x.dtype)


def qk_norm(v, g_nope, g_rope):
    return rmsnorm(v, jnp.concatenate([g_nope, g_rope, g_rope], -1))


def mixer_inputs(x, pos, mix_norm, w_in, q_a_norm, w_q_b, kv_a_norm, q_norm_nope, q_norm_rope):
    h = rmsnorm(x, mix_norm)
    z = h @ w_in
    o1, o2, o3 = POOL_WIDTH, POOL_WIDTH + Q_LORA, POOL_WIDTH + Q_LORA + KV_LORA
    u, q_lat, kv_lat, kpe_raw = z[..., :o1], z[..., o1:o2], z[..., o2:o3], z[..., o3:]
    cos, sin = rope_tables(pos)
    q = (rmsnorm(q_lat, q_a_norm) @ w_q_b).reshape(x.shape[:2] + (N_HEADS, QK_DIM))
    q = jnp.concatenate([q[..., :NOPE_DIM], apply_rope(q[..., NOPE_DIM:], cos, sin)], -1)
    q = qk_norm(q, q_norm_nope, q_norm_rope)
    c_kv = rmsnorm(kv_lat, kv_a_norm)
    k_pe = apply_rope(kpe_raw, cos, sin)
    return u, q, c_kv, k_pe


def expand_kv(c_kv, k_pe, w_kv_b, k_norm_nope, k_norm_rope):
    kv = jnp.einsum('...c,chd->...hd', c_kv, w_kv_b.reshape(KV_LORA, N_HEADS, NOPE_DIM + V_DIM))
    k_nope, v = kv[..., :NOPE_DIM], kv[..., NOPE_DIM:]
    k_pe_h = jnp.broadcast_to(k_pe[..., None, :], k_nope.shape[:-1] + (ROPE_DIM,))
    k = qk_norm(jnp.concatenate([k_nope, k_pe_h], -1), k_norm_nope, k_norm_rope)
    return k, v


def pool_mix(u_ext, n_prefix, pool_w, pool_scale):
    L = u_ext.shape[1]
    cs = jnp.cumsum(u_ext.astype(F32), axis=1)
    count = jnp.arange(1, L + 1, dtype=F32)[None, :, None]
    outs = []
    for g, w in enumerate(POOL_WINDOWS):
        sl = slice(g * POOL_GROUP_DIM, (g + 1) * POOL_GROUP_DIM)
        c = cs[..., sl]
        shifted = jnp.pad(c, ((0, 0), (w, 0), (0, 0)))[:, :L]
        d = (c - shifted) / jnp.minimum(count, float(w)) - u_ext[..., sl].astype(F32)
        d = d[:, n_prefix:].astype(u_ext.dtype)
        outs.append(jnp.einsum('btc,cd->btd', d, pool_w[g]))
    return jnp.concatenate(outs, -1) * pool_scale


def prompt_attention(q, k, v):
    B, T = q.shape[0], q.shape[1]
    n_blk = -(-T // Q_BLOCK)
    Tp = n_blk * Q_BLOCK
    qb = jnp.pad(q, ((0, 0), (0, Tp - T), (0, 0), (0, 0)))
    qb = qb.reshape(B, n_blk, Q_BLOCK, N_HEADS, QK_DIM).transpose(1, 0, 2, 3, 4)
    kpos = jnp.arange(T)

    def block(args):
        i, qi = args
        s = jnp.einsum('bqhd,bkhd->bhqk', qi, k, preferred_element_type=F32) * ATTN_SCALE
        qpos = i * Q_BLOCK + jnp.arange(Q_BLOCK)
        s = jnp.where(kpos[None, :] <= qpos[:, None], s, NEG_INF)
        p = jax.nn.softmax(s, axis=-1).astype(v.dtype)
        return jnp.einsum('bhqk,bkhd->bqhd', p, v)

    out = lax.map(block, (jnp.arange(n_blk), qb))
    return out.transpose(1, 0, 2, 3, 4).reshape(B, Tp, N_HEADS, V_DIM)[:, :T]


def sample_attention(q, c_new, kpe_new, cache_ckv, cache_kpe, layer, page_table,
                     w_kv_b, k_norm_nope, k_norm_rope):
    n_past = page_table.shape[1] * PAGE_SIZE
    S = q.shape[1]

    def one(args):
        qi, ci, pi, pages = args
        past_c = cache_ckv[layer, pages].reshape(n_past, KV_LORA).astype(ci.dtype)
        past_p = cache_kpe[layer, pages].reshape(n_past, ROPE_DIM).astype(pi.dtype)
        c_all = jnp.concatenate([past_c, ci], 0)
        p_all = jnp.concatenate([past_p, pi], 0)
        k, v = expand_kv(c_all, p_all, w_kv_b, k_norm_nope, k_norm_rope)
        s = jnp.einsum('shd,khd->hsk', qi, k, preferred_element_type=F32) * ATTN_SCALE
        kpos = jnp.arange(n_past + S)
        qpos = n_past + jnp.arange(S)
        s = jnp.where(kpos[None, :] <= qpos[:, None], s, NEG_INF)
        p = jax.nn.softmax(s, axis=-1).astype(v.dtype)
        return jnp.einsum('hsk,khd->shd', p, v)

    return lax.map(one, (q, c_new, kpe_new, page_table))


def merge_groups(x, pool_out, attn_out, pool_out_norm, attn_out_norm, w_out):
    a = attn_out.reshape(attn_out.shape[:2] + (ATTN_WIDTH,))
    cat = jnp.concatenate([rmsnorm(pool_out, pool_out_norm), rmsnorm(a, attn_out_norm)], -1)
    return x + cat @ w_out


def setup_inputs(seed: int = 0) -> dict:
    key = jax.random.key(seed)
    ks = iter(jax.random.split(key, 48))

    def nrm(shape, scale):
        return jax.random.normal(next(ks), shape, F32) * scale

    def gain(shape):
        return 1.0 + 0.05 * jax.random.normal(next(ks), shape, F32)

    n_pages = PAST_LEN // PAGE_SIZE
    n_used = DEC_BATCH * n_pages
    n_pool = n_used + max(n_used // 4, 1)
    page_table = jax.random.permutation(next(ks), n_pool)[:n_used].reshape(DEC_BATCH, n_pages).astype(jnp.int32)
    return {
        'x_prompt': nrm((BATCH, SEQ, D_MODEL), 1.0),
        'x_sample': nrm((DEC_BATCH, DEC_SEQ, D_MODEL), 1.0),
        'cache_ckv': nrm((DEPTH, n_pool, PAGE_SIZE, KV_LORA), 1.0),
        'cache_kpe': nrm((DEPTH, n_pool, PAGE_SIZE, ROPE_DIM), 1.0),
        'state_pool': nrm((DEPTH, DEC_BATCH, POOL_STATE, POOL_WIDTH), 1.0),
        'page_table': page_table,
        'meta_tokens': nrm((N_META, D_MODEL), 1.0),
        'ffn1_norm': gain((DEPTH, D_MODEL)),
        'ffn1_w_gate': nrm((DEPTH, D_MODEL, D_FF), D_MODEL ** -0.5),
        'ffn1_w_up': nrm((DEPTH, D_MODEL, D_FF), D_MODEL ** -0.5),
        'ffn1_w_down': nrm((DEPTH, D_FF, D_MODEL), D_FF ** -0.5),
        'mix_norm': gain((DEPTH, D_MODEL)),
        'w_in': nrm((DEPTH, D_MODEL, IN_WIDTH), D_MODEL ** -0.5),
        'pool_w': nrm((DEPTH, POOL_GROUPS, POOL_GROUP_DIM, POOL_GROUP_DIM), POOL_GROUP_DIM ** -0.5),
        'pool_scale': gain((DEPTH, POOL_WIDTH)),
        'q_a_norm': gain((DEPTH, Q_LORA)),
        'w_q_b': nrm((DEPTH, Q_LORA, N_HEADS * QK_DIM), Q_LORA ** -0.5),
        'kv_a_norm': gain((DEPTH, KV_LORA)),
        'w_kv_b': nrm((DEPTH, KV_LORA, N_HEADS * (NOPE_DIM + V_DIM)), KV_LORA ** -0.5),
        'q_norm_nope': gain((DEPTH, NOPE_DIM)),
        'q_norm_rope': gain((DEPTH, ROPE_HALF)),
        'k_norm_nope': gain((DEPTH, NOPE_DIM)),
        'k_norm_rope': gain((DEPTH, ROPE_HALF)),
        'pool_out_norm': gain((DEPTH, POOL_WIDTH)),
        'attn_out_norm': gain((DEPTH, ATTN_WIDTH)),
        'w_out': nrm((DEPTH, MIX_WIDTH, D_MODEL), MIX_WIDTH ** -0.5),
        'ffn2_norm': gain((DEPTH, D_MODEL)),
        'ffn2_w_gate': nrm((DEPTH, D_MODEL, D_FF), D_MODEL ** -0.5),
        'ffn2_w_up': nrm((DEPTH, D_MODEL, D_FF), D_MODEL ** -0.5),
        'ffn2_w_down': nrm((DEPTH, D_FF, D_MODEL), D_FF ** -0.5),
    }


def reference(x_prompt, x_sample, cache_ckv, cache_kpe, state_pool, page_table,
              meta_tokens, ffn1_norm, ffn1_w_gate, ffn1_w_up, ffn1_w_down,
              mix_norm, w_in, pool_w, pool_scale, q_a_norm, w_q_b, kv_a_norm, w_kv_b,
              q_norm_nope, q_norm_rope, k_norm_nope, k_norm_rope,
              pool_out_norm, attn_out_norm, w_out,
              ffn2_norm, ffn2_w_gate, ffn2_w_up, ffn2_w_down):
    B = x_prompt.shape[0]
    T = N_META + x_prompt.shape[1]
    meta = jnp.broadcast_to(meta_tokens[None].astype(x_prompt.dtype), (B, N_META, D_MODEL))
    xp = jnp.concatenate([meta, x_prompt], axis=1)
    xs = x_sample
    n_past = page_table.shape[1] * PAGE_SIZE
    pos_p = jnp.arange(T)
    pos_s = n_past + jnp.arange(x_sample.shape[1])

    ckv_p, kpe_p, pool_p, ckv_s, kpe_s, pool_s = [], [], [], [], [], []
    for l in range(DEPTH):
        xp = swiglu_half(xp, ffn1_norm[l], ffn1_w_gate[l], ffn1_w_up[l], ffn1_w_down[l])
        xs = swiglu_half(xs, ffn1_norm[l], ffn1_w_gate[l], ffn1_w_up[l], ffn1_w_down[l])

        u, q, c, kpe = mixer_inputs(xp, pos_p, mix_norm[l], w_in[l], q_a_norm[l], w_q_b[l],
                                    kv_a_norm[l], q_norm_nope[l], q_norm_rope[l])
        k, v = expand_kv(c, kpe, w_kv_b[l], k_norm_nope[l], k_norm_rope[l])
        attn = prompt_attention(q, k, v)
        pool = pool_mix(u, 0, pool_w[l], pool_scale[l])
        xp = merge_groups(xp, pool, attn, pool_out_norm[l], attn_out_norm[l], w_out[l])
        ckv_p.append(c)
        kpe_p.append(kpe)
        pool_p.append(u[:, -POOL_STATE:])

        us, qs, cs, kpes = mixer_inputs(xs, pos_s, mix_norm[l], w_in[l], q_a_norm[l], w_q_b[l],
                                        kv_a_norm[l], q_norm_nope[l], q_norm_rope[l])
        attn_s = sample_attention(qs, cs, kpes, cache_ckv, cache_kpe, l, page_table,
                                  w_kv_b[l], k_norm_nope[l], k_norm_rope[l])
        u_ext = jnp.concatenate([state_pool[l].astype(us.dtype), us], axis=1)
        pool_sm = pool_mix(u_ext, POOL_STATE, pool_w[l], pool_scale[l])
        xs = merge_groups(xs, pool_sm, attn_s, pool_out_norm[l], attn_out_norm[l], w_out[l])
        ckv_s.append(cs)
        kpe_s.append(kpes)
        pool_s.append(u_ext[:, -POOL_STATE:])

        xp = swiglu_half(xp, ffn2_norm[l], ffn2_w_gate[l], ffn2_w_up[l], ffn2_w_down[l])
        xs = swiglu_half(xs, ffn2_norm[l], ffn2_w_gate[l], ffn2_w_up[l], ffn2_w_down[l])

    y_prompt = xp[:, N_META:]
    y_sample = xs
    return (y_prompt, y_sample,
            jnp.stack(ckv_p), jnp.stack(kpe_p), jnp.stack(pool_p),
            jnp.stack(ckv_s), jnp.stack(kpe_s), jnp.stack(pool_s))
```

```python
from contextlib import ExitStack
import numpy as np
import concourse.bass as bass
import concourse.mybir as mybir
from concourse.bass_utils import run_bass_kernel_spmd

F32 = mybir.dt.float32
BF16 = mybir.dt.bfloat16
I32 = mybir.dt.int32
AF = mybir.ActivationFunctionType
ALU = mybir.AluOpType
AX = mybir.AxisListType

D_MODEL = 2048
NCH = 16
N_META = 16
NHALO = 30
POOL_WIDTH = 1024
POOL_WINDOWS = (2, 4, 8, 16)
POOL_STATE = 15
N_HEADS = 8
NOPE = 128
ROPE = 64
QK_DIM = 192
V_DIM = 128
Q_LORA = 512
KV_LORA = 256
IN_WIDTH = POOL_WIDTH + Q_LORA + KV_LORA + ROPE
EPS = 1e-6
ROPE_THETA = 10000.0
ATTN_SCALE = QK_DIM ** -0.5
DEC_SEQ = 4
PAGE = 128

FULL_CFG = dict(SPAN=1024, NSEQ=16, NPAGES=64, DFF=5632, NPOOLPG=10240, DEPTH=2, BATCH=2)

ENGS = ("pe", "act", "dve", "pool", "sp")
SEM_LIMIT = 8000
DMA_LIMIT = 500


class Instr:
    __slots__ = ("eng", "fn", "waits", "signal", "idx", "dma_key", "vc", "clock", "gid")


class Prog:
    def __init__(self):
        self.per_eng = {e: [] for e in ENGS}
        self.last_w = {}
        self.readers = {}
        self.known = {e: {} for e in ENGS}
        self.dma_count = {}
        self.last_dma = {}
        self.gid = 0

    def _wait_on(self, eng, J, waits):
        known = self.known[eng]
        ck, cv = J.clock
        if J.eng == eng and J.dma_key is None and eng == "pe":
            return
        if known.get(ck, 0) >= cv:
            return
        waits.append(J)
        J.signal = True
        changed = dict(known)
        for k, v in J.vc.items():
            if changed.get(k, 0) < v:
                changed[k] = v
        if changed.get(ck, 0) < cv:
            changed[ck] = cv
        self.known[eng] = changed

    def op(self, eng, fn, reads=(), writes=(), dma_key=None):
        I = Instr()
        I.eng = eng
        I.fn = fn
        I.signal = False
        I.dma_key = dma_key
        I.gid = self.gid
        self.gid += 1
        deps = {}
        for r in reads:
            w = self.last_w.get(r)
            if w is not None:
                deps[w.gid] = w
        for w_ in writes:
            w = self.last_w.get(w_)
            if w is not None:
                deps[w.gid] = w
            for rd in self.readers.get(w_, ()):
                deps[rd.gid] = rd
        waits = []
        for g in sorted(deps, reverse=True):
            self._wait_on(eng, deps[g], waits)
        I.waits = waits
        lst = self.per_eng[eng]
        I.idx = len(lst)
        lst.append(I)
        if dma_key is None:
            I.clock = (eng, I.idx + 1)
        else:
            c = self.dma_count.get(dma_key, 0) + 1
            self.dma_count[dma_key] = c
            I.clock = (("dma", dma_key), c)
            self.last_dma[dma_key] = I
        I.vc = self.known[eng]
        for r in reads:
            self.readers.setdefault(r, []).append(I)
        for w_ in writes:
            self.last_w[w_] = I
            self.readers[w_] = []
        return I

    def barrier(self):
        lasts = []
        for e in ENGS:
            for J in reversed(self.per_eng[e]):
                if J.dma_key is None and J.fn is not None:
                    lasts.append(J)
                    break
        lasts += list(self.last_dma.values())
        for e in ENGS:
            I = Instr()
            I.eng = e
            I.fn = None
            I.signal = False
            I.dma_key = None
            I.gid = self.gid
            self.gid += 1
            waits = []
            for J in sorted(lasts, key=lambda j: -j.gid):
                self._wait_on(e, J, waits)
            I.waits = waits
            lst = self.per_eng[e]
            I.idx = len(lst)
            lst.append(I)
            I.clock = (e, I.idx + 1)
            I.vc = self.known[e]

    def emit(self, nc, new_sem):
        sig_of = {}
        for e in ENGS:
            cnt = 0
            semi = 0
            for I in self.per_eng[e]:
                if I.dma_key is not None or I.fn is None:
                    continue
                if I.signal:
                    if cnt >= SEM_LIMIT:
                        semi += 1
                        cnt = 0
                    cnt += 1
                    sig_of[I.gid] = ((e, semi), cnt)
        sems = {}

        def get_sem(key):
            if key not in sems:
                sems[key] = new_sem("s%d" % len(sems))
            return sems[key]

        def emit_eng(e, engobj):
            for I in self.per_eng[e]:
                for J in I.waits:
                    if J.dma_key is not None:
                        inc = DMA_INC.get(J.dma_key, 16)
                        ep, cnt = divmod(J.clock[1] - 1, DMA_LIMIT)
                        if ep > 0:
                            engobj.wait_ge(get_sem(("dma", J.dma_key, ep - 1)), inc * DMA_LIMIT)
                        engobj.wait_ge(get_sem(("dma", J.dma_key, ep)), inc * (cnt + 1))
                    else:
                        sk, cv = sig_of[J.gid]
                        engobj.wait_ge(get_sem(sk), cv)
                if I.fn is None:
                    continue
                ins = I.fn(engobj)
                if I.dma_key is not None:
                    ins.then_inc(get_sem(("dma", I.dma_key, (I.clock[1] - 1) // DMA_LIMIT)), DMA_INC.get(I.dma_key, 16))
                elif I.signal:
                    sk, cv = sig_of[I.gid]
                    ins.then_inc(get_sem(sk), 1)

        with nc.Block() as block:
            @block.tensor
            def _(eng):
                emit_eng("pe", eng)

            @block.scalar
            def _(eng):
                emit_eng("act", eng)

            @block.vector
            def _(eng):
                emit_eng("dve", eng)

            @block.gpsimd
            def _(eng):
                emit_eng("pool", eng)

            @block.sync
            def _(eng):
                emit_eng("sp", eng)
        return len(sems)


DMA_INC = {}


class Arena:
    def __init__(self, t, words, base=0):
        self.t = t
        self.base = base
        self.words = base + words
        self.off = base
        self.peak = base

    def mark(self):
        return self.off

    def reset(self, m):
        self.off = m

    def room(self, shape, dtype):
        n = 1
        for s in shape:
            n *= s
        esz = 2 if dtype == BF16 else 4
        words = ((n * esz + 3) // 4 + 7) // 8 * 8
        return self.off + words <= self.words

    def alloc(self, shape, dtype, parts=128):
        n = 1
        for s in shape:
            n *= s
        esz = 2 if dtype == BF16 else 4
        words = (n * esz + 3) // 4
        words = (words + 7) // 8 * 8
        assert self.off + words <= self.words, ("SBUF arena overflow", self.off, words, self.words)
        ap = self.t[0:parts, self.off:self.off + words]
        self.off += words
        self.peak = max(self.peak, self.off)
        if dtype != F32:
            ap = ap.bitcast(dtype)
        ap = ap[:, 0:n]
        if len(shape) == 2:
            return ap.rearrange("p (a b) -> p a b", b=shape[1])
        if len(shape) == 3:
            return ap.rearrange("p (a b c) -> p a b c", b=shape[1], c=shape[2])
        return ap


def tiles_of(n, maxw):
    k = -(-n // maxw)
    base = -(-n // k)
    out = []
    s = 0
    while s < n:
        w = min(base, n - s)
        out.append((s, w))
        s += w
    return out


def build(cfg, stop_after=None):
    SPAN, NSEQ, NPAGES, DFF, NPOOLPG, DEPTH = (cfg[k] for k in ("SPAN", "NSEQ", "NPAGES", "DFF", "NPOOLPG", "DEPTH"))
    NS = NSEQ * DEC_SEQ
    C_M, C_P, C_H = 0, N_META, N_META + SPAN
    C_S = C_H + NHALO
    NT = C_S + NS
    NQ = C_S
    NR = N_META + SPAN
    NCHK = max(1, SPAN // 256)
    CP = SPAN // NCHK
    CH_A = [0 if q == 0 else N_META + q * CP for q in range(NCHK)]
    CH_B = [N_META + (q + 1) * CP for q in range(NCHK)]
    CH_R = [CH_B[q] - CH_A[q] for q in range(NCHK)]
    KBS = [(0, 0, 16)]
    for r in range(4):
        for i in range(SPAN // 128):
            q = (i * 128) // CP
            KBS.append((q, r * CH_R[q] + (N_META + i * 128 - CH_A[q]), 128))
    NKB = len(KBS)
    U8 = mybir.dt.uint8
    NFC = DFF // 128
    GS = 4
    assert NFC % GS == 0
    NG = NFC // GS
    TT = tiles_of(NT, 512)
    TB = tiles_of(NT, 128)
    TB = [(s, min(128, NT - s)) for s in range(0, NT, 128)]

    nc = bass.Bass("TRN2", target_bir_lowering=False)

    def din(name, shape, dt=F32):
        return nc.dram_tensor(name, list(shape), dt, kind="ExternalInput").ap()

    def dout(name, shape, dt=F32):
        return nc.dram_tensor(name, list(shape), dt, kind="ExternalOutput").ap()

    x_tok = din("x_tok", [NT, D_MODEL])
    ident_d = din("ident", [128, 128])
    w = {}
    for nm in ("ffn1", "ffn2"):
        w[nm + "_w_gate"] = din(nm + "_w_gate", [DEPTH, D_MODEL, DFF])
        w[nm + "_w_up"] = din(nm + "_w_up", [DEPTH, D_MODEL, DFF])
        w[nm + "_w_down"] = din(nm + "_w_down", [DEPTH, DFF, D_MODEL])
        w[nm + "_normT"] = din(nm + "_normT", [DEPTH, 128, NCH])
    y_tok = dout("y_tok", [NT, D_MODEL])
    w_in_d = din("w_in", [DEPTH, D_MODEL, IN_WIDTH])
    w_qb_d = din("w_q_b", [DEPTH, Q_LORA, N_HEADS * QK_DIM])
    w_kvb_d = din("w_kv_b", [DEPTH, KV_LORA, N_HEADS * 256])
    pool_w_d = din("pool_w", [DEPTH, 4, 256, 256])
    w_out_d = din("w_out", [DEPTH, D_MODEL, D_MODEL])
    mix_normT_d = din("mix_normT", [DEPTH, 128, NCH])
    vecs_d = din("vecs", [DEPTH, 128, 32])
    kvnorm_d = din("kvnorm_b", [DEPTH, 128, 256])
    qpos_d = din("qpos_b", [128, NQ])
    hv_d = din("hvtab", [128, NCH * NHALO])
    invc_d = din("invcnt", [128, 4 * 46])
    mknew_d = din("mknew", [128, 32])
    kpos_d = din("kpos_t", [128, NKB])
    cosT_d = din("cosT", [64, NT])
    sinT_d = din("sinT", [64, NT])
    cost_d = din("cos_tok", [NT, 32])
    sint_d = din("sin_tok", [NT, 32])
    cckv_d = [din("cache_ckv%d" % l, [NPOOLPG * 128, 256]).rearrange("n d -> (n d)") for l in range(DEPTH)]
    ckpe_d = [din("cache_kpe%d" % l, [NPOOLPG * 128, 64]).rearrange("n d -> (n d)") for l in range(DEPTH)]
    state_d = din("state_pool", [DEPTH, NSEQ * 15, POOL_WIDTH])
    pt_d = din("page_table", [1, NSEQ * NPAGES], I32)
    o_ckv_p = dout("o_ckv_p", [DEPTH, NR, 256])
    o_kpe_p = dout("o_kpe_p", [DEPTH, NR, 64])
    o_pool_p = dout("o_pool_p", [DEPTH, 15, POOL_WIDTH])
    o_ckv_s = dout("o_ckv_s", [DEPTH, NS, 256])
    o_kpe_s = dout("o_kpe_s", [DEPTH, NS, 64])
    o_pool_s = dout("o_pool_s", [DEPTH, NSEQ * 15, POOL_WIDTH])
    cc_in = [[nc.dram_tensor("cc_in%d_%d" % (l, q), [CH_R[q], 320], F32, kind="Internal").ap() for q in range(NCHK)] for l in range(DEPTH)]
    cc_out = [[nc.dram_tensor("cc_out%d_%d" % (l, q), [4 * CH_R[q], 320], F32, kind="Internal", addr_space="Local").ap() for q in range(NCHK)]
              for l in range(DEPTH)]
    kvS = [nc.dram_tensor("kvS%d" % l, [NS, 320], F32, kind="Internal").ap() for l in range(DEPTH)]
    ctok_d = [nc.dram_tensor("ctokd%d" % l, [NKB * 128, 256], BF16, kind="Internal").ap() for l in range(DEPTH)]

    P = Prog()
    es = ExitStack()
    with es:
        WORDS = 53100
        arena_t = es.enter_context(nc.sbuf_tensor("arena", [128, WORDS], F32))
        A = Arena(arena_t, WORDS)
        psum = [es.enter_context(nc.psum_tensor("ps%d" % i, [128, 512], F32)) for i in range(8)]

        def PS(i):
            return ("ps", i)

        xT = A.alloc([NCH, NT], F32)
        xn_off = A.off
        xnT = A.alloc([NCH, NT], BF16)
        xn_words = A.off - xn_off
        ident_f = A.alloc([1, 128], F32)[:, 0, :]
        ident_b = A.alloc([1, 128], BF16)[:, 0, :]
        ones_b = A.alloc([1, 128], BF16)[:, 0, :]
        eps_t = A.alloc([1, 8], F32)[:, 0, :]
        NSTG = 3
        st_off = A.off
        stage = [A.alloc([1, 2048], F32)[:, 0, :] for _ in range(NSTG)]
        st_i = [0]

        def next_stage():
            i = st_i[0] % NSTG
            st_i[0] += 1
            return i

        cast_rr = [0]
        REGS = {}

        def cast_eng():
            e = ("act", "pool", "dve")[cast_rr[0] % 3]
            cast_rr[0] += 1
            return e

        def cast(eng, out, in_, reads, writes):
            if eng == "act":
                P.op("act", lambda e: e.activation(out=out, in_=in_, func=AF.Copy), reads, writes)
            else:
                P.op(eng, lambda e: e.tensor_copy(out=out, in_=in_), reads, writes)

        P.op("sp", lambda e: e.dma_start(out=ident_f, in_=ident_d), [], ["ident_f"], dma_key="const")
        P.op("dve", lambda e: e.tensor_copy(out=ident_b, in_=ident_f), ["ident_f"], ["ident_b"])
        P.op("dve", lambda e: e.memset(ones_b, 1.0), [], ["ones_b"])
        P.op("dve", lambda e: e.memset(eps_t, EPS), [], ["eps"])

        for bi, (t0, tw) in enumerate(TB):
            si = next_stage()
            P.op("sp", (lambda e, si=si, t0=t0, tw=tw: e.dma_start(out=stage[si][0:tw, :], in_=x_tok[t0:t0 + tw, :])),
                 [], [("stage", si)], dma_key=("stage", si))
            for c4 in range(NCH // 4):
                bank = c4 % 2
                for cc in range(4):
                    c = c4 * 4 + cc
                    P.op("pe", (lambda e, bank=bank, cc=cc, c=c, si=si, tw=tw: e.transpose(
                        out=psum[bank][:, cc * 128:cc * 128 + tw], in_=stage[si][0:tw, c * 128:(c + 1) * 128],
                        identity=ident_f[0:tw, 0:tw])), [("stage", si), "ident_f"], [PS(bank)])
                eng = "dve" if c4 % 2 == 0 else "act"
                src = (lambda bank=bank, tw=tw: psum[bank][:, :].rearrange("p (a b) -> p a b", b=128)[:, :, 0:tw])
                dst = (lambda c4=c4, t0=t0, tw=tw: xT[:, c4 * 4:c4 * 4 + 4, t0:t0 + tw])
                if eng == "dve":
                    P.op("dve", (lambda e, src=src, dst=dst: e.tensor_copy(out=dst(), in_=src())),
                         [PS(bank)], [("xT", c, bi) for c in range(c4 * 4, c4 * 4 + 4)])
                else:
                    P.op("act", (lambda e, src=src, dst=dst: e.activation(out=dst(), in_=src(), func=AF.Copy)),
                         [PS(bank)], [("xT", c, bi) for c in range(c4 * 4, c4 * 4 + 4)])

        def xT_keys(c, t0, tw):
            b0, b1 = t0 // 128, (t0 + tw - 1) // 128
            return [("xT", c, b) for b in range(b0, b1 + 1)]

        def all_xT_keys(t0, tw):
            return [k for c in range(NCH) for k in xT_keys(c, t0, tw)]

        def rmsnorm_full(gT_dram):
            m = A.mark()
            gT = A.alloc([1, NCH], F32)[:, 0, :]
            P.op("sp", lambda e: e.dma_start(out=gT, in_=gT_dram), [], ["gT"], dma_key="gT")
            sq = A.alloc([NCH, 512], BF16)
            tmp = A.alloc([1, 512], F32)[:, 0, :]
            rstd = A.alloc([1, 512], F32)[:, 0, :]
            for ti, (t0, tw) in enumerate(TT):
                P.op("act", (lambda e, t0=t0, tw=tw: e.activation(out=sq[:, :, 0:tw], in_=xT[:, :, t0:t0 + tw], func=AF.Square)),
                     all_xT_keys(t0, tw), ["n_sq"])
                for c in range(NCH):
                    P.op("pe", (lambda e, c=c, tw=tw: e.matmul(psum[7][:, 0:tw], lhsT=ones_b, rhs=sq[:, c, 0:tw],
                                                             start=(c == 0), stop=(c == NCH - 1))),
                         ["n_sq", "ones_b"], [PS(7)])
                P.op("act", (lambda e, tw=tw: e.activation(out=tmp[:, 0:tw], in_=psum[7][:, 0:tw], func=AF.Sqrt,
                                                          bias=eps_t[:, 0:1], scale=1.0 / D_MODEL)),
                     [PS(7), "eps"], ["n_tmp"])
                P.op("dve", (lambda e, tw=tw: e.reciprocal(out=rstd[:, 0:tw], in_=tmp[:, 0:tw])), ["n_tmp"], ["n_rstd"])
                for c in range(NCH):
                    P.op("dve", (lambda e, c=c, t0=t0, tw=tw: e.scalar_tensor_tensor(
                        out=xnT[:, c, t0:t0 + tw], in0=xT[:, c, t0:t0 + tw], scalar=gT[:, c:c + 1], in1=rstd[:, 0:tw],
                        op0=ALU.mult, op1=ALU.mult)),
                         xT_keys(c, t0, tw) + ["gT", "n_rstd"], [("xnT", c, ti)])
            P.barrier()
            A.reset(m)

        def ffn(l, nm):
            wg_d = w[nm + "_w_gate"][l].rearrange("(c p) f -> p c f", p=128)
            wu_d = w[nm + "_w_up"][l].rearrange("(c p) f -> p c f", p=128)
            wd_d = w[nm + "_w_down"][l]
            rmsnorm_full(w[nm + "_normT"][l])
            m = A.mark()
            NWB = 2
            wg_b = [A.alloc([NCH, 128], BF16) for _ in range(NWB)]
            wu_b = [A.alloc([NCH, 128], BF16) for _ in range(NWB)]
            NWD = 6
            wd_b = [A.alloc([1, D_MODEL], BF16)[:, 0, :] for _ in range(NWD)]
            aT = [A.alloc([GS, NT], BF16) for _ in range(2)]
            sil = [A.alloc([1, 512], F32)[:, 0, :] for _ in range(2)]
            cnt = dict(wb=0, wd=0, sil=0, gu=0, dn=0)

            def up_phase(g):
                ab = g % 2
                for jj in range(GS):
                    j = g * GS + jj
                    wb = cnt["wb"] % NWB
                    cnt["wb"] += 1
                    for (wd_, wb_t, nmk) in ((wg_d, wg_b, "wg"), (wu_d, wu_b, "wu")):
                        si = next_stage()
                        P.op("sp", (lambda e, si=si, wd_=wd_, j=j: e.dma_start(
                            out=stage[si].rearrange("p (c f) -> p c f", f=128), in_=wd_[:, :, j * 128:(j + 1) * 128])),
                             [], [("stage", si)], dma_key=("stage", si))
                        cast(cast_eng(), wb_t[wb].rearrange("p c f -> p (c f)"), stage[si], [("stage", si)], [(nmk, wb)])
                    for ti, (t0, tw) in enumerate(TT):
                        bg = (cnt["gu"] % 2) * 2
                        cnt["gu"] += 1
                        for (wb_t, nmk, bank) in ((wg_b, "wg", bg), (wu_b, "wu", bg + 1)):
                            for c in range(NCH):
                                P.op("pe", (lambda e, wb_t=wb_t, wb=wb, c=c, t0=t0, tw=tw, bank=bank: e.matmul(
                                    psum[bank][:, 0:tw], lhsT=wb_t[wb][:, c, :], rhs=xnT[:, c, t0:t0 + tw],
                                    start=(c == 0), stop=(c == NCH - 1))),
                                     [(nmk, wb), ("xnT", c, ti)], [PS(bank)])
                        sb = cnt["sil"] % 2
                        cnt["sil"] += 1
                        P.op("act", (lambda e, sb=sb, tw=tw, bg=bg: e.activation(out=sil[sb][:, 0:tw], in_=psum[bg][:, 0:tw], func=AF.Silu)),
                             [PS(bg)], [("sil", sb)])
                        P.op("dve", (lambda e, sb=sb, tw=tw, bg=bg, ab=ab, jj=jj, t0=t0: e.tensor_tensor(
                            out=aT[ab][:, jj, t0:t0 + tw], in0=sil[sb][:, 0:tw], in1=psum[bg + 1][:, 0:tw], op=ALU.mult)),
                             [("sil", sb), PS(bg + 1)], [("aT", ab, jj, ti)])

            def down_phase(g):
                ab = g % 2
                slots = []
                for jj in range(GS):
                    j = g * GS + jj
                    ws = cnt["wd"] % NWD
                    cnt["wd"] += 1
                    slots.append(ws)
                    si = next_stage()
                    P.op("sp", (lambda e, si=si, j=j: e.dma_start(out=stage[si], in_=wd_d[j * 128:(j + 1) * 128, :])),
                         [], [("stage", si)], dma_key=("stage", si))
                    cast(cast_eng(), wd_b[ws], stage[si], [("stage", si)], [("wd", ws)])
                for i in range(NCH):
                    for ti, (t0, tw) in enumerate(TT):
                        bank = 4 + cnt["dn"] % 2
                        cnt["dn"] += 1
                        for jj in range(GS):
                            P.op("pe", (lambda e, bank=bank, jj=jj, i=i, t0=t0, tw=tw, ws=slots[jj]: e.matmul(
                                psum[bank][:, 0:tw], lhsT=wd_b[ws][:, i * 128:(i + 1) * 128], rhs=aT[ab][:, jj, t0:t0 + tw],
                                start=(jj == 0), stop=(jj == GS - 1))),
                                 [("wd", slots[jj]), ("aT", ab, jj, ti)], [PS(bank)])
                        P.op("dve", (lambda e, bank=bank, i=i, t0=t0, tw=tw: e.scalar_tensor_tensor(
                            out=xT[:, i, t0:t0 + tw], in0=psum[bank][:, 0:tw], scalar=0.5, in1=xT[:, i, t0:t0 + tw],
                            op0=ALU.mult, op1=ALU.add)),
                             [PS(bank)] + xT_keys(i, t0, tw), xT_keys(i, t0, tw))

            up_phase(0)
            for g in range(NG):
                if g + 1 < NG:
                    up_phase(g + 1)
                down_phase(g)
            P.barrier()
            A.reset(m)

        def op(eng, method, reads, writes, dma_key=None, **kw):
            return P.op(eng, (lambda e: getattr(e, method)(**kw)), reads, writes, dma_key=dma_key)

        def xn_keys(c, t0, tw):
            return [("xnT", c, ti) for ti, (a, b) in enumerate(TT) if a < t0 + tw and t0 < a + b]

        def seg_iter(t0, tw):
            for nm, a, b in (("M", C_M, C_P), ("P", C_P, C_H), ("H", C_H, C_S), ("S", C_S, NT)):
                lo, hi = max(t0, a), min(t0 + tw, b)
                if lo < hi:
                    yield nm, lo - a, lo - t0, hi - lo

        def ld(dst, src, n, key, f=None):
            si = next_stage()
            sv = stage[si][:, 0:n]
            if f is not None:
                sv = sv.rearrange("p (c f) -> p c f", f=f)
            op("sp", "dma_start", [], [("stage", si)], dma_key=("stage", si), out=sv, in_=src)
            cast(cast_eng(), dst, sv, [("stage", si)], [key])

        def ldf(dst, src, key, eng="sp"):
            op(eng, "dma_start", [], [key], dma_key=("ldf", key), out=dst, in_=src)

        def mixer(l):
            m_mix = A.mark()
            w_in_v = w_in_d[l].rearrange("(c p) f -> p c f", p=128)
            rmsnorm_full(mix_normT_d[l])
            vecs = A.alloc([1, 32], F32)[:, 0, :]
            ldf(vecs, vecs_d[l], "vecs")
            kvn = A.alloc([1, 256], F32)[:, 0, :]
            ldf(kvn, kvnorm_d[l], "kvn")
            b192 = A.alloc([1, 8], F32)[:, 0, :]
            op("dve", "memset", [], ["b192"], ap=b192, constant=192.0 * EPS)
            gqk = A.alloc([1, 8], F32)[:, 0, :]
            op("dve", "tensor_tensor", ["vecs"], ["gqk"], out=gqk[:, 0:1], in0=vecs[:, 28:29], in1=vecs[:, 29:30], op=ALU.mult)
            op("dve", "tensor_tensor", ["vecs", "gqk"], ["gqk"], out=gqk[:, 1:2], in0=vecs[:, 30:31], in1=vecs[:, 31:32], op=ALU.mult)

            m_b = A.mark()
            wkvin = A.alloc([NCH, 320], BF16)
            for q4 in range(4):
                ld(wkvin[:, q4 * 4:(q4 + 1) * 4, :], w_in_v[:, q4 * 4:(q4 + 1) * 4, 1536:1856], 1280, ("wkvin", q4), f=320)
            kvo = [A.alloc([1, 320], F32)[:, 0, :] for _ in range(2)]
            junk = A.alloc([1, 256], F32)[:, 0, :]
            sm = A.alloc([1, 8], F32)[:, 0, :]
            ct = [A.alloc([1, 32], F32)[:, 0, :] for _ in range(2)]
            st_ = [A.alloc([1, 32], F32)[:, 0, :] for _ in range(2)]
            rt = A.alloc([4, 32], F32)
            ccin_keys = []
            for bi, (t0, tw) in enumerate(TB):
                kb_ = bi % 2
                ps = psum[bi % 2]
                op("sp", "dma_start", [], [("ct", kb_)], dma_key=("ct", kb_), out=ct[kb_][0:tw, :], in_=cost_d[t0:t0 + tw, :])
                op("sp", "dma_start", [], [("st", kb_)], dma_key=("st", kb_), out=st_[kb_][0:tw, :], in_=sint_d[t0:t0 + tw, :])
                for c in range(NCH):
                    op("pe", "matmul", xn_keys(c, t0, tw) + [("wkvin", c // 4)], [PS(bi % 2)], out=ps[0:tw, 0:320],
                       lhsT=xnT[:, c, t0:t0 + tw], rhs=wkvin[:, c, :], start=(c == 0), stop=(c == NCH - 1))
                op("act", "activation", [PS(bi % 2)], ["junk"], out=junk[0:tw, :], in_=ps[0:tw, 0:256], func=AF.Square)
                op("dve", "tensor_reduce", ["junk"], ["sm0"], out=sm[0:tw, 0:1], in_=junk[0:tw, :], axis=AX.X, op=ALU.add)
                op("act", "activation", ["sm0", "eps"], ["sm1"], out=sm[0:tw, 1:2], in_=sm[0:tw, 0:1], func=AF.Sqrt, bias=eps_t[0:tw, 0:1], scale=1.0 / 256)
                op("dve", "reciprocal", ["sm1"], ["sm2"], out=sm[0:tw, 2:3], in_=sm[0:tw, 1:2])
                ko = kvo[kb_]
                op("dve", "scalar_tensor_tensor", [PS(bi % 2), "sm2", "kvn"], [("kvo", kb_)], out=ko[0:tw, 0:256], in0=ps[0:tw, 0:256],
                   scalar=sm[0:tw, 2:3], in1=kvn[0:tw, :], op0=ALU.mult, op1=ALU.mult)
                x1, x2 = ps[0:tw, 256:288], ps[0:tw, 288:320]
                op("dve", "tensor_tensor", [PS(bi % 2), ("ct", kb_)], ["rt0"], out=rt[0:tw, 0, :], in0=x1, in1=ct[kb_][0:tw, :], op=ALU.mult)
                op("dve", "tensor_tensor", [PS(bi % 2), ("st", kb_)], ["rt1"], out=rt[0:tw, 1, :], in0=x2, in1=st_[kb_][0:tw, :], op=ALU.mult)
                op("dve", "tensor_tensor", [PS(bi % 2), ("st", kb_)], ["rt2"], out=rt[0:tw, 2, :], in0=x1, in1=st_[kb_][0:tw, :], op=ALU.mult)
                op("dve", "tensor_tensor", [PS(bi % 2), ("ct", kb_)], ["rt3"], out=rt[0:tw, 3, :], in0=x2, in1=ct[kb_][0:tw, :], op=ALU.mult)
                op("dve", "tensor_tensor", ["rt0", "rt1"], [("kvo", kb_)], out=ko[0:tw, 256:288], in0=rt[0:tw, 0, :], in1=rt[0:tw, 1, :], op=ALU.subtract)
                op("dve", "tensor_tensor", ["rt2", "rt3"], [("kvo", kb_)], out=ko[0:tw, 288:320], in0=rt[0:tw, 2, :], in1=rt[0:tw, 3, :], op=ALU.add)
                for nm, so, src, n in seg_iter(t0, tw):
                    if nm == "H":
                        continue
                    dk = ("kvo", kb_)
                    if nm in ("M", "P"):
                        ro = so + (0 if nm == "M" else N_META)
                        op("sp", "dma_start", [dk], [("o_ckv_p", l, bi, nm)], dma_key=dk, out=o_ckv_p[l][ro:ro + n, :], in_=ko[src:src + n, 0:256])
                        op("sp", "dma_start", [dk], [("o_kpe_p", l, bi, nm)], dma_key=dk, out=o_kpe_p[l][ro:ro + n, :], in_=ko[src:src + n, 256:320])
                        for q in range(NCHK):
                            lo, hi = max(ro, CH_A[q]), min(ro + n, CH_B[q])
                            if lo < hi:
                                kk = ("ccin", l, bi, nm, q)
                                op("sp", "dma_start", [dk], [kk], dma_key=dk, out=cc_in[l][q][lo - CH_A[q]:hi - CH_A[q], :],
                                   in_=ko[src + lo - ro:src + hi - ro, :])
                                ccin_keys.append((q, kk))
                    else:
                        op("sp", "dma_start", [dk], [("o_ckv_s", l, bi)], dma_key=dk, out=o_ckv_s[l][so:so + n, :], in_=ko[src:src + n, 0:256])
                        op("sp", "dma_start", [dk], [("o_kpe_s", l, bi)], dma_key=dk, out=o_kpe_s[l][so:so + n, :], in_=ko[src:src + n, 256:320])
                        op("sp", "dma_start", [dk], [("kvS", l)], dma_key=dk, out=kvS[l][so:so + n, :], in_=ko[src:src + n, :])
            for q in range(NCHK):
                DMA_INC[("cc", l, q)] = 1
                op("pool", "collective_compute", [kk for (qq, kk) in ccin_keys if qq == q], [("ccout", l, q)], dma_key=("cc", l, q), kind="AllGather",
                   op=ALU.bypass, replica_groups=[[0, 1, 2, 3], [4, 5, 6, 7]], ins=[cc_in[l][q]], outs=[cc_out[l][q]])
            P.barrier()
            A.reset(m_b)

            qlat = A.alloc([4, NT], BF16)
            m_d = A.mark()
            winq = A.alloc([NCH, 512], BF16)
            for mt in range(4):
                ld(winq[:, :, mt * 128:(mt + 1) * 128], w_in_v[:, :, 1024 + mt * 128:1024 + (mt + 1) * 128], 2048, ("winq", mt), f=128)
            qraw = A.alloc([4, 512], BF16)
            sq4 = A.alloc([4, 512], BF16)
            tmpf = A.alloc([2, 512], F32)
            for ti, (t0, tw) in enumerate(TT):
                for mt in range(4):
                    bank = mt % 2
                    for c in range(NCH):
                        op("pe", "matmul", [("winq", mt), ("xnT", c, ti)], [PS(bank)], out=psum[bank][:, 0:tw],
                           lhsT=winq[:, c, mt * 128:(mt + 1) * 128], rhs=xnT[:, c, t0:t0 + tw], start=(c == 0), stop=(c == NCH - 1))
                    op("act", "activation", [PS(bank)], [("qraw", mt)], out=qraw[:, mt, 0:tw], in_=psum[bank][:, 0:tw], func=AF.Copy)
                    op("act", "activation", [PS(bank)], [("sq4", mt)], out=sq4[:, mt, 0:tw], in_=psum[bank][:, 0:tw], func=AF.Square)
                for mt in range(4):
                    op("pe", "matmul", [("sq4", mt), "ones_b"], [PS(7)], out=psum[7][:, 0:tw], lhsT=ones_b, rhs=sq4[:, mt, 0:tw],
                       start=(mt == 0), stop=(mt == 3))
                op("act", "activation", [PS(7), "eps"], ["tmpf0"], out=tmpf[:, 0, 0:tw], in_=psum[7][:, 0:tw], func=AF.Sqrt,
                   bias=eps_t[:, 0:1], scale=1.0 / Q_LORA)
                op("dve", "reciprocal", ["tmpf0"], ["tmpf1"], out=tmpf[:, 1, 0:tw], in_=tmpf[:, 0, 0:tw])
                for mt in range(4):
                    op("dve", "scalar_tensor_tensor", [("qraw", mt), "vecs", "tmpf1"], [("qlat", mt, ti)], out=qlat[:, mt, t0:t0 + tw],
                       in0=qraw[:, mt, 0:tw], scalar=vecs[:, mt:mt + 1], in1=tmpf[:, 1, 0:tw], op0=ALU.mult, op1=ALU.mult)
            P.barrier()
            A.reset(m_d)

            m_e = A.mark()
            UW = 76 + SPAN
            UZ, UM, UH, UP = 0, 15, 46, 76
            poolraw = A.alloc([8, NT], BF16)
            poolw = A.alloc([8, 256], BF16)
            ld(poolw, pool_w_d[l].rearrange("g (cc p) d -> p (g cc) d", p=128), 2048, "poolw", f=256)
            invc = A.alloc([4, 46], F32)
            ldf(invc, invc_d.rearrange("p (g n) -> p g n", n=46), "invc")
            ub = [A.alloc([1, UW], F32)[:, 0, :] for _ in range(3)]
            us = [A.alloc([NSEQ, 19], F32) for _ in range(3)]
            dT = A.alloc([2, NT], BF16)
            wt = [A.alloc([NCH, 128], BF16) for _ in range(2)]
            t46 = A.alloc([1, 46], F32)[:, 0, :]
            NSH = -(-NSEQ // 8)
            sstt = [A.alloc([1, 128], F32)[:, 0, :] for _ in range(2)]
            opt = [A.alloc([1, 128], F32)[:, 0, :] for _ in range(2)]
            ost = [A.alloc([1, 128], F32)[:, 0, :] for _ in range(2)]
            rot = dict(s=0, p=0, o=0)
            for u3 in ub:
                op("dve", "memset", [], ["ub0", "ub1", "ub2"], ap=u3[:, 0:UH], constant=0.0)
            for g in range(4):
                wnd = POOL_WINDOWS[g]
                for cc in range(2):
                    ch = g * 2 + cc
                    wb = ch % 2
                    ld(wt[wb], w_in_v[:, :, ch * 128:(ch + 1) * 128], 2048, ("wt", wb), f=128)
                    u0, u1, u2 = ub
                    s0, s1, s2 = us
                    for hf in range(NSH):
                        ns = min(8, NSEQ - hf * 8)
                        sb = rot["s"] % 2
                        rot["s"] += 1
                        op("sp", "dma_start", [], [("sstt", sb)], dma_key=("sstt", sb), out=sstt[sb][0:ns * 15, :],
                           in_=state_d[l][hf * 120:hf * 120 + ns * 15, ch * 128:(ch + 1) * 128])
                        op("pe", "transpose", [("sstt", sb), "ident_f"], [PS(6)], out=psum[6][:, 0:ns * 15],
                           in_=sstt[sb][0:ns * 15, :], identity=ident_f[0:ns * 15, 0:ns * 15])
                        op("dve", "tensor_copy", [PS(6)], ["us0"], out=s0[:, hf * 8:hf * 8 + ns, 0:15],
                           in_=psum[6][:, 0:ns * 15].rearrange("p (s k) -> p s k", k=15))
                    for ti, (t0, tw) in enumerate(TT):
                        bank = ti % 2
                        for c in range(NCH):
                            op("pe", "matmul", [("wt", wb), ("xnT", c, ti)], [PS(bank)], out=psum[bank][:, 0:tw],
                               lhsT=wt[wb][:, c, :], rhs=xnT[:, c, t0:t0 + tw], start=(c == 0), stop=(c == NCH - 1))
                        for nm, so, src, n in seg_iter(t0, tw):
                            if nm == "S":
                                s_a, s_b = so // 4, (so + n) // 4
                                op("act", "activation", [PS(bank)], ["us0"], out=s0[:, s_a:s_b, 15:19],
                                   in_=psum[bank][:, src:src + n].rearrange("p (s k) -> p s k", k=4), func=AF.Copy)
                            else:
                                uo = {"M": UM, "H": UH, "P": UP}[nm] + so
                                op("act", "activation", [PS(bank)], ["ub0"], out=u0[:, uo:uo + n], in_=psum[bank][:, src:src + n], func=AF.Copy)
                    pb_ = rot["p"] % 2
                    rot["p"] += 1
                    op("pe", "transpose", ["ub0", "ident_f"], [PS(6)], out=psum[6][0:15, 0:128], in_=u0[:, UW - 15:UW], identity=ident_f)
                    op("dve", "tensor_copy", [PS(6)], [("opt", pb_)], out=opt[pb_][0:15, :], in_=psum[6][0:15, 0:128])
                    op("sp", "dma_start", [("opt", pb_)], [("o_pool_p", l, ch)], dma_key=("opt", pb_), out=o_pool_p[l][:, ch * 128:(ch + 1) * 128], in_=opt[pb_][0:15, :])
                    for hf in range(NSH):
                        ns = min(8, NSEQ - hf * 8)
                        ob = rot["o"] % 2
                        rot["o"] += 1
                        op("dve", "tensor_copy", ["us0", "ub2"], ["ub2"], out=ub[2][:, 0:ns * 15].rearrange("p (s k) -> p s k", k=15), in_=s0[:, hf * 8:hf * 8 + ns, 4:19])
                        op("pe", "transpose", ["ub2", "ident_f"], [PS(6)], out=psum[6][0:ns * 15, 0:128], in_=ub[2][:, 0:ns * 15], identity=ident_f)
                        op("dve", "tensor_copy", [PS(6)], [("ost", ob)], out=ost[ob][0:ns * 15, :], in_=psum[6][0:ns * 15, 0:128])
                        op("sp", "dma_start", [("ost", ob)], [("o_pool_s", l, hf, ch)], dma_key=("ost", ob),
                           out=o_pool_s[l][hf * 120:hf * 120 + ns * 15, ch * 128:(ch + 1) * 128], in_=ost[ob][0:ns * 15, :])
                    cur, nxt = u0, u1
                    scur, snxt = s0, s1
                    k = 1
                    while k < wnd:
                        op("dve", "tensor_copy", ["ub0", "ub1", "ub2"], ["ub1", "ub2"], out=nxt[:, 0:k], in_=cur[:, 0:k])
                        op("dve", "tensor_tensor", ["ub0", "ub1", "ub2"], ["ub1", "ub2"], out=nxt[:, k:UW], in0=cur[:, k:UW], in1=cur[:, 0:UW - k], op=ALU.add)
                        op("dve", "tensor_tensor", ["us0", "us1", "us2d"], ["us1", "us2d"], out=snxt[:, :, k:19], in0=scur[:, :, k:19], in1=scur[:, :, 0:19 - k], op=ALU.add)
                        if cur is u0:
                            cur, nxt = u1, u2
                            scur, snxt = s1, s2
                        else:
                            cur, nxt = nxt, cur
                            scur, snxt = snxt, scur
                        k *= 2
                    allu = ["ub0", "ub1", "ub2", "us0", "us1", "us2d"]
                    op("dve", "tensor_tensor", allu + ["invc"], ["t46"], out=t46[:, 0:16], in0=cur[:, UM:UM + 16], in1=invc[:, g, 0:16], op=ALU.mult)
                    op("dve", "tensor_tensor", allu + ["invc", "t46"], ["t46"], out=t46[:, 16:46], in0=cur[:, UH:UH + 30], in1=invc[:, g, 16:46], op=ALU.mult)
                    op("dve", "tensor_tensor", allu + ["t46"], [("dT", cc)], out=dT[:, cc, C_M:C_M + 16], in0=t46[:, 0:16], in1=u0[:, UM:UM + 16], op=ALU.subtract)
                    op("dve", "tensor_tensor", allu + ["t46"], [("dT", cc)], out=dT[:, cc, C_H:C_H + 30], in0=t46[:, 16:46], in1=u0[:, UH:UH + 30], op=ALU.subtract)
                    op("dve", "scalar_tensor_tensor", allu, [("dT", cc)], out=dT[:, cc, C_P:C_P + SPAN], in0=cur[:, UP:UP + SPAN], scalar=1.0 / wnd,
                       in1=u0[:, UP:UP + SPAN], op0=ALU.mult, op1=ALU.subtract)
                    op("dve", "scalar_tensor_tensor", allu, [("dT", cc)], out=dT[:, cc, C_S:NT].rearrange("p (s k) -> p s k", k=4),
                       in0=scur[:, :, 15:19], scalar=1.0 / wnd, in1=s0[:, :, 15:19], op0=ALU.mult, op1=ALU.subtract)
                for dd in range(2):
                    chd = g * 2 + dd
                    for ti, (t0, tw) in enumerate(TT):
                        bank = 2 + ti % 2
                        for cc in range(2):
                            op("pe", "matmul", ["poolw", ("dT", cc)], [PS(bank)], out=psum[bank][:, 0:tw],
                               lhsT=poolw[:, g * 2 + cc, dd * 128:(dd + 1) * 128], rhs=dT[:, cc, t0:t0 + tw], start=(cc == 0), stop=(cc == 1))
                        op("act", "activation", [PS(bank), "vecs"], [("poolraw", chd, ti)], out=poolraw[:, chd, t0:t0 + tw], in_=psum[bank][:, 0:tw],
                           func=AF.Copy, scale=vecs[:, 20 + chd:21 + chd])
            P.barrier()
            sq1 = [A.alloc([1, 512], BF16)[:, 0, :] for _ in range(2)]
            pn = A.alloc([2, 512], F32)
            for ti, (t0, tw) in enumerate(TT):
                for ch in range(8):
                    op("act", "activation", [("poolraw", ch, ti)], [("sq1", ch % 2)], out=sq1[ch % 2][:, 0:tw], in_=poolraw[:, ch, t0:t0 + tw], func=AF.Square)
                    op("pe", "matmul", [("sq1", ch % 2), "ones_b"], [PS(7)], out=psum[7][:, 0:tw], lhsT=ones_b, rhs=sq1[ch % 2][:, 0:tw], start=(ch == 0), stop=(ch == 7))
                op("act", "activation", [PS(7), "eps"], ["pn0"], out=pn[:, 0, 0:tw], in_=psum[7][:, 0:tw], func=AF.Sqrt, bias=eps_t[:, 0:1], scale=1.0 / POOL_WIDTH)
                op("dve", "reciprocal", ["pn0"], ["pn1"], out=pn[:, 1, 0:tw], in_=pn[:, 0, 0:tw])
                for ch in range(8):
                    op("dve", "scalar_tensor_tensor", [("poolraw", ch, ti), "vecs", "pn1"], [("poolraw", ch, ti)], out=poolraw[:, ch, t0:t0 + tw],
                       in0=poolraw[:, ch, t0:t0 + tw], scalar=vecs[:, 4 + ch:5 + ch], in1=pn[:, 1, 0:tw], op0=ALU.mult, op1=ALU.mult)
            wout_half(l, 0, poolraw, "poolraw", wt)
            P.barrier()
            A.reset(m_e)

            heads(l, qlat, vecs, gqk, b192)
            P.barrier()
            A.reset(m_mix)

        def wout_half(l, half, cat, catname, wt):
            wo_v = w_out_d[l].rearrange("(c p) f -> p c f", p=128)
            for i in range(NCH):
                wb = i % 2
                ld(wt[wb][:, 0:8, :], wo_v[:, half * 8:half * 8 + 8, i * 128:(i + 1) * 128], 1024, ("wt", wb), f=128)
                for ti, (t0, tw) in enumerate(TT):
                    bank = 2 + (i * len(TT) + ti) % 2
                    for c in range(8):
                        op("pe", "matmul", [("wt", wb), (catname, c, ti)], [PS(bank)], out=psum[bank][:, 0:tw], lhsT=wt[wb][:, c, :],
                           rhs=cat[:, c, t0:t0 + tw], start=(c == 0), stop=(c == 7))
                    op("dve", "tensor_tensor", [PS(bank)] + xT_keys(i, t0, tw), xT_keys(i, t0, tw), out=xT[:, i, t0:t0 + tw],
                       in0=psum[bank][:, 0:tw], in1=xT[:, i, t0:t0 + tw], op=ALU.add)

        def heads(l, qlat, vecs, gqk, b192):
            XA = Arena(arena_t, xn_words, base=xn_off)
            SA = Arena(arena_t, NSTG * 2048, base=st_off)

            def xalloc(shape, dt):
                return XA.alloc(shape, dt) if XA.room(shape, dt) else A.alloc(shape, dt)

            attn = XA.alloc([8, NT], BF16)
            wqb = xalloc([4, 1536], BF16)
            wqrot = xalloc([4, 512], BF16)
            for c in range(4):
                ld(wqb[:, c, :], w_qb_d[l][c * 128:(c + 1) * 128, :], 1536, ("wqb", c))
            for c in range(4):
                src = wqb[:, c, :].rearrange("p (h d) -> p h d", d=192)
                dst = wqrot[:, c, :].rearrange("p (h j) -> p h j", j=64)
                op("dve", "tensor_scalar", [("wqb", c)], [("wqrot", c)], out=dst[:, :, 0:32], in0=src[:, :, 160:192], scalar1=-1.0, scalar2=None, op0=ALU.mult)
                op("dve", "tensor_copy", [("wqb", c)], [("wqrot", c)], out=dst[:, :, 32:64], in_=src[:, :, 128:160])
            wnope = A.alloc([2, 1024], BF16)
            wv = A.alloc([2, 1024], BF16)
            wnT = A.alloc([8, 256], BF16)
            for cc in range(2):
                si = next_stage()
                op("sp", "dma_start", [], [("stage", si)], dma_key=("stage", si), out=stage[si], in_=w_kvb_d[l][cc * 128:(cc + 1) * 128, :])
                v3 = stage[si].rearrange("p (h d) -> p h d", d=256)
                op("dve", "tensor_copy", [("stage", si)], [("wnope", cc)], out=wnope[:, cc, :].rearrange("p (h d) -> p h d", d=128), in_=v3[:, :, 0:128])
                op("pool", "tensor_copy", [("stage", si)], [("wv", cc)], out=wv[:, cc, :].rearrange("p (h d) -> p h d", d=128), in_=v3[:, :, 128:256])
            psb = [psum[i][:, :].bitcast(BF16) for i in range(8)]
            for h in range(N_HEADS):
                for cc in range(2):
                    op("pe", "transpose", [("wnope", cc), "ident_b"], [PS(0)], out=psb[0][:, 0:128], in_=wnope[:, cc, h * 128:(h + 1) * 128], identity=ident_b)
                    op("dve", "tensor_copy", [PS(0)], [("wnT", h)], out=wnT[:, h, cc * 128:(cc + 1) * 128], in_=psb[0][:, 0:128])
            qabS = A.alloc([2, NSEQ * 32], BF16)
            qrS = A.alloc([1, NSEQ * 32], BF16)[:, 0, :]
            sqk2 = [A.alloc([1, 1024], BF16)[:, 0, :] for _ in range(2)]
            j642 = [A.alloc([1, 64], F32)[:, 0, :] for _ in range(2)]
            s82 = [A.alloc([4, 8], F32) for _ in range(2)]
            P.barrier()
            m_keys = A.mark()

            def prep(n, srcf, skey, cb, cT, kT, rstd, okeys, par, tb, kn0):
                sqk_, j64_, s8_ = sqk2[par], j642[par], s82[par]
                op("act", "activation", [skey], [okeys[0]], out=cb[0:n, :], in_=srcf[0:n, :], func=AF.Copy)
                for cc in range(2):
                    op("pe", "transpose", [okeys[0], "ident_b"], [PS(tb)], out=psb[tb][:, cc * 128:cc * 128 + n], in_=cb[0:n, cc * 128:(cc + 1) * 128],
                       identity=ident_b[0:n, 0:n])
                op("pe", "transpose", [okeys[0], "ident_b"], [PS(tb)], out=psb[tb][0:64, 256:256 + n], in_=cb[0:n, 256:320], identity=ident_b[0:n, 0:n])
                op("dve", "tensor_copy", [PS(tb)], [okeys[1]], out=cT[:, :, 0:n], in_=psb[tb][:, 0:256].rearrange("p (c k) -> p c k", k=128)[:, :, 0:n])
                op("dve", "tensor_copy", [PS(tb)], [okeys[2]], out=kT[0:64, 0:n], in_=psb[tb][0:64, 256:256 + n])
                for half in range(2):
                    for cc in range(2):
                        op("pe", "matmul", [okeys[1], ("wnope", cc)], [PS(kn0 + half)], out=psum[kn0 + half][0:n, 0:512], lhsT=cT[:, cc, 0:n],
                           rhs=wnope[:, cc, half * 512:(half + 1) * 512], start=(cc == 0), stop=(cc == 1))
                    op("act", "activation", [PS(kn0 + half)], [("sqk", par)], out=sqk_[0:n, half * 512:(half + 1) * 512], in_=psum[kn0 + half][0:n, 0:512], func=AF.Square)
                op("dve", "tensor_reduce", [("sqk", par)], [("s8a", par)], out=s8_[0:n, 0, :], in_=sqk_[0:n, :].rearrange("p (h d) -> p h d", d=128), axis=AX.X, op=ALU.add)
                op("act", "activation", [skey], [("j64", par)], out=j64_[0:n, :], in_=srcf[0:n, 256:320], func=AF.Square)
                op("dve", "tensor_reduce", [("j64", par)], [("s8b", par)], out=s8_[0:n, 1, 0:1], in_=j64_[0:n, :], axis=AX.X, op=ALU.add)
                op("dve", "tensor_scalar", [("s8a", par), ("s8b", par)], [("s8c", par)], out=s8_[0:n, 2, :], in0=s8_[0:n, 0, :], scalar1=s8_[0:n, 1, 0:1], scalar2=None, op0=ALU.add)
                op("act", "activation", [("s8c", par), "b192"], [("s8d", par)], out=s8_[0:n, 3, :], in_=s8_[0:n, 2, :], func=AF.Ln, bias=b192[0:n, 0:1], scale=1.0)
                op("act", "activation", [("s8d", par)], [okeys[3]], out=rstd[0:n, :], in_=s8_[0:n, 3, :], func=AF.Exp, scale=-0.5)

            kpos = A.alloc([1, NKB], F32)[:, 0, :]
            ldf(kpos, kpos_d, "kpos")
            cT_all = A.alloc([2, NKB * 128], BF16)
            kT_all = xalloc([1, NKB * 128], BF16)[:, 0, :]
            rstd_all = xalloc([NKB, 8], F32)
            kst = [xalloc([1, 320], F32)[:, 0, :] for _ in range(2)]
            pck = [A.alloc([1, 320], BF16)[:, 0, :] for _ in range(2)]
            for kb, (q, r0, n) in enumerate(KBS):
                b = kb % 2
                op("sp", "dma_start", [("ccout", l, q)], [("kst", b)], dma_key=("kst", b), out=kst[b][0:n, :], in_=cc_out[l][q][r0:r0 + n, :])
                prep(n, kst[b], ("kst", b), pck[b], cT_all[:, :, kb * 128:(kb + 1) * 128], kT_all[:, kb * 128:(kb + 1) * 128], rstd_all[:, kb, :],
                     [("pck", b), ("cT", kb), ("kT", kb), ("rstd", kb)], b, b * 3, b * 3 + 1)
                op("sp", "dma_start", [("pck", b)], [("ctokd", l, kb)], dma_key=("pck", b), out=ctok_d[l][kb * 128:kb * 128 + n, :], in_=pck[b][0:n, 0:256])
            tq = SA.alloc([2, 512], F32)
            rr = SA.alloc([2, 512], F32)
            sqn = SA.alloc([2, 512], BF16)
            qne = SA.alloc([1, 512], BF16)[:, 0, :]
            qre = SA.alloc([1, 512], BF16)[:, 0, :]
            qab = SA.alloc([2, 512], BF16)
            olat = SA.alloc([2, 512], BF16)
            pT = [SA.alloc([1, 512], BF16)[:, 0, :] for _ in range(2)]
            mk = [SA.alloc([1, 512], BF16)[:, 0, :] for _ in range(2)]
            pTm = [SA.alloc([1, 512], BF16)[:, 0, :] for _ in range(2)]
            rden = SA.alloc([1, 512], F32)[:, 0, :]
            NCB = 4
            cbuf = [A.alloc([1, 256], BF16)[:, 0, :] for _ in range(NCB)]
            cosb = A.alloc([1, 512], F32)[:, 0, :]
            sinb = A.alloc([1, 512], F32)[:, 0, :]
            qposb = A.alloc([1, 512], F32)[:, 0, :]
            cbi = [0]
            for ti, (t0, tw) in enumerate(TT):
                qw = min(t0 + tw, NQ) - t0
                op("sp", "dma_start", [], ["cosb"], dma_key="cosb", out=cosb[0:64, 0:tw], in_=cosT_d[:, t0:t0 + tw])
                op("sp", "dma_start", [], ["sinb"], dma_key="sinb", out=sinb[0:64, 0:tw], in_=sinT_d[:, t0:t0 + tw])
                if qw > 0:
                    op("sp", "dma_start", [], ["qposb"], dma_key="qposb", out=qposb[:, 0:qw], in_=qpos_d[:, t0:t0 + qw])
                for h in range(N_HEADS):
                    for c in range(4):
                        op("pe", "matmul", [("wqb", c), ("qlat", c, ti)], [PS(0)], out=psum[0][:, 0:tw], lhsT=wqb[:, c, h * 192:h * 192 + 128],
                           rhs=qlat[:, c, t0:t0 + tw], start=(c == 0), stop=(c == 3))
                    for c in range(4):
                        op("pe", "matmul", [("wqb", c), ("qlat", c, ti)], [PS(1)], out=psum[1][0:64, 0:tw], lhsT=wqb[:, c, h * 192 + 128:h * 192 + 192],
                           rhs=qlat[:, c, t0:t0 + tw], start=(c == 0), stop=(c == 3))
                    for c in range(4):
                        op("pe", "matmul", [("wqrot", c), ("qlat", c, ti)], [PS(2)], out=psum[2][0:64, 0:tw], lhsT=wqrot[:, c, h * 64:(h + 1) * 64],
                           rhs=qlat[:, c, t0:t0 + tw], start=(c == 0), stop=(c == 3))
                    op("dve", "tensor_tensor", [PS(1), "cosb"], ["tq0"], out=tq[0:64, 0, 0:tw], in0=psum[1][0:64, 0:tw], in1=cosb[0:64, 0:tw], op=ALU.mult)
                    op("dve", "tensor_tensor", [PS(2), "sinb"], ["tq1"], out=tq[0:64, 1, 0:tw], in0=psum[2][0:64, 0:tw], in1=sinb[0:64, 0:tw], op=ALU.mult)
                    op("dve", "tensor_tensor", ["tq0", "tq1"], ["tq0"], out=tq[0:64, 0, 0:tw], in0=tq[0:64, 0, 0:tw], in1=tq[0:64, 1, 0:tw], op=ALU.add)
                    op("act", "activation", [PS(0)], ["sqn0"], out=sqn[:, 0, 0:tw], in_=psum[0][:, 0:tw], func=AF.Square)
                    op("act", "activation", ["tq0"], ["sqn1"], out=sqn[0:64, 1, 0:tw], in_=tq[0:64, 0, 0:tw], func=AF.Square)
                    op("pe", "matmul", ["sqn0", "ones_b"], [PS(3)], out=psum[3][:, 0:tw], lhsT=ones_b, rhs=sqn[:, 0, 0:tw], start=True, stop=False)
                    op("pe", "matmul", ["sqn1", "ones_b"], [PS(3)], out=psum[3][:, 0:tw], lhsT=ones_b[0:64, :], rhs=sqn[0:64, 1, 0:tw], start=False, stop=True)
                    op("act", "activation", [PS(3), "eps"], ["rr0"], out=rr[:, 0, 0:tw], in_=psum[3][:, 0:tw], func=AF.Sqrt, bias=eps_t[:, 0:1], scale=1.0 / QK_DIM)
                    op("dve", "reciprocal", ["rr0"], ["rr1"], out=rr[:, 1, 0:tw], in_=rr[:, 0, 0:tw])
                    op("dve", "scalar_tensor_tensor", [PS(0), "gqk", "rr1"], ["qne"], out=qne[:, 0:tw], in0=psum[0][:, 0:tw], scalar=gqk[:, 0:1], in1=rr[:, 1, 0:tw],
                       op0=ALU.mult, op1=ALU.mult)
                    op("dve", "scalar_tensor_tensor", ["tq0", "gqk", "rr1"], ["qre"], out=qre[0:64, 0:tw], in0=tq[0:64, 0, 0:tw], scalar=gqk[0:64, 1:2], in1=rr[0:64, 1, 0:tw],
                       op0=ALU.mult, op1=ALU.mult)
                    for cc in range(2):
                        op("pe", "matmul", ["qne", ("wnT", h)], [PS(4 + cc)], out=psum[4 + cc][:, 0:tw], lhsT=wnT[:, h, cc * 128:(cc + 1) * 128], rhs=qne[:, 0:tw],
                           start=True, stop=True)
                        op("act", "activation", [PS(4 + cc)], [("qab", cc)], out=qab[:, cc, 0:tw], in_=psum[4 + cc][:, 0:tw], func=AF.Copy)
                    for nm, so, src, n in seg_iter(t0, tw):
                        if nm == "S":
                            s_a, s_b = so // 4, (so + n) // 4
                            for cc in range(2):
                                op("pool", "tensor_copy", [("qab", cc)], [("qabS", h)],
                                   out=qabS[:, cc, :].rearrange("p (s h t) -> p s h t", h=8, t=4)[:, s_a:s_b, h, :],
                                   in_=qab[:, cc, src:src + n].rearrange("p (s t) -> p s t", t=4))
                            op("pool", "tensor_copy", ["qre"], [("qrS", h)], out=qrS[0:64, :].rearrange("p (s h t) -> p s h t", h=8, t=4)[:, s_a:s_b, h, :],
                               in_=qre[0:64, src:src + n].rearrange("p (s t) -> p s t", t=4))
                    if qw <= 0:
                        continue
                    for kb, (q_, r0, n) in enumerate(KBS):
                        sb = kb % 2
                        ks = slice(kb * 128, kb * 128 + n)
                        cb = cbi[0] % NCB
                        cbi[0] += 1
                        op("sp", "dma_start", [("ctokd", l, kb)], [("cbuf", cb)], dma_key=("cbuf", cb), out=cbuf[cb][0:n, :], in_=ctok_d[l][kb * 128:kb * 128 + n, :])
                        op("pe", "matmul", [("cT", kb), ("qab", 0)], [PS(sb)], out=psum[sb][0:n, 0:qw], lhsT=cT_all[:, 0, ks], rhs=qab[:, 0, 0:qw], start=True, stop=False)
                        op("pe", "matmul", [("cT", kb), ("qab", 1)], [PS(sb)], out=psum[sb][0:n, 0:qw], lhsT=cT_all[:, 1, ks], rhs=qab[:, 1, 0:qw], start=False, stop=False)
                        op("pe", "matmul", [("kT", kb), "qre"], [PS(sb)], out=psum[sb][0:n, 0:qw], lhsT=kT_all[0:64, ks], rhs=qre[0:64, 0:qw], start=False, stop=True)
                        op("act", "activation", [PS(sb), ("rstd", kb)], [("pT", sb)], out=pT[sb][0:n, 0:qw], in_=psum[sb][0:n, 0:qw], func=AF.Exp,
                           scale=rstd_all[0:n, kb, h:h + 1])
                        op("dve", "tensor_scalar", ["qposb", "kpos"], [("mk", sb)], out=mk[sb][0:n, 0:qw], in0=qposb[0:n, 0:qw], scalar1=kpos[0:n, kb:kb + 1],
                           scalar2=None, op0=ALU.is_ge)
                        op("pool", "tensor_tensor", [("pT", sb), ("mk", sb)], [("pTm", sb)], out=pTm[sb][0:n, 0:qw], in0=pT[sb][0:n, 0:qw], in1=mk[sb][0:n, 0:qw], op=ALU.mult)
                        first, last = (kb == 0), (kb == NKB - 1)
                        for cc in range(2):
                            op("pe", "matmul", [("cbuf", cb), ("pTm", sb)], [PS(2 + cc)], out=psum[2 + cc][:, 0:qw], lhsT=cbuf[cb][0:n, cc * 128:(cc + 1) * 128],
                               rhs=pTm[sb][0:n, 0:qw], start=first, stop=last)
                        op("pe", "matmul", ["ones_b", ("pTm", sb)], [PS(6)], out=psum[6][:, 0:qw], lhsT=ones_b[0:n, :], rhs=pTm[sb][0:n, 0:qw], start=first, stop=last)
                    op("dve", "reciprocal", [PS(6)], ["rden"], out=rden[:, 0:qw], in_=psum[6][:, 0:qw])
                    for cc in range(2):
                        op("dve", "tensor_tensor", [PS(2 + cc), "rden"], [("olat", cc)], out=olat[:, cc, 0:qw], in0=psum[2 + cc][:, 0:qw], in1=rden[:, 0:qw], op=ALU.mult)
                    for cc in range(2):
                        op("pe", "matmul", [("olat", cc), ("wv", cc)], [PS(7)], out=psum[7][:, 0:qw], lhsT=wv[:, cc, h * 128:(h + 1) * 128], rhs=olat[:, cc, 0:qw],
                           start=(cc == 0), stop=(cc == 1))
                    op("act", "activation", [PS(7)], [("attn", h, ti)], out=attn[:, h, t0:t0 + qw], in_=psum[7][:, 0:qw], func=AF.Copy)
            P.barrier()
            A.reset(m_keys)
            ptb = A.alloc([1, NSEQ * NPAGES], I32)[0:1, 0, :]
            ldf(ptb, pt_d, "ptb")
            mknew = A.alloc([1, 32], BF16)[:, 0, :]
            mknew_f = A.alloc([1, 32], F32)[:, 0, :]
            ldf(mknew_f, mknew_d, "mknew_f")
            op("dve", "tensor_copy", ["mknew_f"], ["mknew"], out=mknew, in_=mknew_f)
            pst = [A.alloc([1, 320], F32)[:, 0, :] for _ in range(2)]
            pctok = [A.alloc([1, 320], BF16)[:, 0, :] for _ in range(2)]
            pcT = [A.alloc([2, 128], BF16) for _ in range(2)]
            pkT = [A.alloc([1, 128], BF16)[:, 0, :] for _ in range(2)]
            prs = [A.alloc([1, 8], F32)[:, 0, :] for _ in range(2)]
            sc2 = [A.alloc([1, 32], F32)[:, 0, :] for _ in range(2)]
            pTs = [A.alloc([1, 32], BF16)[:, 0, :] for _ in range(2)]
            rdS = A.alloc([1, 32], F32)[:, 0, :]
            olS = A.alloc([2, 32], BF16)
            regs = REGS
            def gather(e, s, pg, b, which):
                if "pg" not in regs:
                    regs["pg"] = e.alloc_register("r_pg")
                    regs["c"] = e.alloc_register("r_c")
                e.reg_load(regs["pg"], ptb[0:1, s * NPAGES + pg:s * NPAGES + pg + 1])
                if which == 0:
                    e.reg_mul(regs["c"], regs["pg"], 128 * 256 * 4)
                    rc = bass.RuntimeValue.of_register_unchecked(regs["c"])
                    return e.dma_start(out=pst[b][:, 0:256].bitcast(U8), in_=cckv_d[l].bitcast(U8)[bass.ds(rc, 128 * 1024)].rearrange("(p d) -> p d", d=1024))
                e.reg_mul(regs["c"], regs["pg"], 128 * 64 * 4)
                rc = bass.RuntimeValue.of_register_unchecked(regs["c"])
                return e.dma_start(out=pst[b][:, 256:320].bitcast(U8), in_=ckpe_d[l].bitcast(U8)[bass.ds(rc, 128 * 256)].rearrange("(p d) -> p d", d=256))

            for s in range(NSEQ):
                for pg in range(NPAGES + 1):
                    b = (s * (NPAGES + 1) + pg) % 2
                    if pg < NPAGES:
                        n = 128
                        P.op("sp", (lambda e, s=s, pg=pg, b=b: gather(e, s, pg, b, 0)), ["ptb"], [("pst", b)], dma_key=("pst", b))
                        P.op("sp", (lambda e, s=s, pg=pg, b=b: gather(e, s, pg, b, 1)), ["ptb"], [("pst", b)], dma_key=("pst", b))
                    else:
                        n = DEC_SEQ
                        op("sp", "dma_start", [("kvS", l)], [("pst", b)], dma_key=("pst", b), out=pst[b][0:n, :], in_=kvS[l][s * 4:(s + 1) * 4, :])
                    prep(n, pst[b], ("pst", b), pctok[b], pcT[b], pkT[b], prs[b], [("pctok", b), ("pcT", b), ("pkT", b), ("prs", b)], b, (0 if b == 0 else 7), 1)
                    sc = sc2[b]
                    so = b * 32
                    op("pe", "matmul", [("pcT", b)] + [("qabS", h) for h in range(8)], [PS(3)], out=psum[3][0:n, so:so + 32], lhsT=pcT[b][:, 0, 0:n],
                       rhs=qabS[:, 0, s * 32:(s + 1) * 32], start=True, stop=False)
                    op("pe", "matmul", [("pcT", b)], [PS(3)], out=psum[3][0:n, so:so + 32], lhsT=pcT[b][:, 1, 0:n], rhs=qabS[:, 1, s * 32:(s + 1) * 32], start=False, stop=False)
                    op("pe", "matmul", [("pkT", b)] + [("qrS", h) for h in range(8)], [PS(3)], out=psum[3][0:n, so:so + 32], lhsT=pkT[b][0:64, 0:n],
                       rhs=qrS[0:64, s * 32:(s + 1) * 32], start=False, stop=True)
                    op("dve", "tensor_tensor", [PS(3), ("prs", b)], [("sc", b)], out=sc[0:n, :].rearrange("p (h t) -> p h t", t=4),
                       in0=psum[3][0:n, so:so + 32].rearrange("p (h t) -> p h t", t=4), in1=prs[b][0:n, :].unsqueeze(2).broadcast_to([n, 8, 4]), op=ALU.mult)
                    op("act", "activation", [("sc", b)], [("pTs", b)], out=pTs[b][0:n, :], in_=sc[0:n, :], func=AF.Exp)
                    if pg == NPAGES:
                        op("dve", "tensor_tensor", [("pTs", b), "mknew"], [("pTs", b)], out=pTs[b][0:n, :], in0=pTs[b][0:n, :], in1=mknew[0:n, :], op=ALU.mult)
                    first, last = (pg == 0), (pg == NPAGES)
                    for cc in range(2):
                        op("pe", "matmul", [("pctok", b), ("pTs", b)], [PS(4 + cc)], out=psum[4 + cc][:, 0:32], lhsT=pctok[b][0:n, cc * 128:(cc + 1) * 128],
                           rhs=pTs[b][0:n, :], start=first, stop=last)
                    op("pe", "matmul", ["ones_b", ("pTs", b)], [PS(6)], out=psum[6][:, 0:32], lhsT=ones_b[0:n, :], rhs=pTs[b][0:n, :], start=first, stop=last)
                op("dve", "reciprocal", [PS(6)], ["rdS"], out=rdS, in_=psum[6][:, 0:32])
                for cc in range(2):
                    op("dve", "tensor_tensor", [PS(4 + cc), "rdS"], [("olS", cc)], out=olS[:, cc, :], in0=psum[4 + cc][:, 0:32], in1=rdS, op=ALU.mult)
                for h in range(N_HEADS):
                    for cc in range(2):
                        op("pe", "matmul", [("olS", cc), ("wv", cc)], [PS(3)], out=psum[3][:, 128 + h * 4:128 + (h + 1) * 4], lhsT=wv[:, cc, h * 128:(h + 1) * 128],
                           rhs=olS[:, cc, h * 4:(h + 1) * 4], start=(cc == 0), stop=(cc == 1))
                tis = [ti for ti, (a, bb) in enumerate(TT) if a < C_S + (s + 1) * 4 and C_S + s * 4 < a + bb]
                op("act", "activation", [PS(3)], [("attn", h, ti) for h in range(8) for ti in tis], out=attn[:, :, C_S + s * 4:C_S + (s + 1) * 4],
                   in_=psum[3][:, 128:160].rearrange("p (h t) -> p h t", t=4), func=AF.Copy)
            P.barrier()
            A.reset(m_keys)
            sq1 = [A.alloc([1, 512], BF16)[:, 0, :] for _ in range(2)]
            nr = A.alloc([2, 512], F32)
            for ti, (t0, tw) in enumerate(TT):
                for ch in range(8):
                    op("act", "activation", [("attn", ch, ti)], [("sq1", ch % 2)], out=sq1[ch % 2][:, 0:tw], in_=attn[:, ch, t0:t0 + tw], func=AF.Square)
                    op("pe", "matmul", [("sq1", ch % 2), "ones_b"], [PS(7)], out=psum[7][:, 0:tw], lhsT=ones_b, rhs=sq1[ch % 2][:, 0:tw], start=(ch == 0), stop=(ch == 7))
                op("act", "activation", [PS(7), "eps"], ["nr0"], out=nr[:, 0, 0:tw], in_=psum[7][:, 0:tw], func=AF.Sqrt, bias=eps_t[:, 0:1], scale=1.0 / 1024)
                op("dve", "reciprocal", ["nr0"], ["nr1"], out=nr[:, 1, 0:tw], in_=nr[:, 0, 0:tw])
                for ch in range(8):
                    op("dve", "scalar_tensor_tensor", [("attn", ch, ti), "vecs", "nr1"], [("attn", ch, ti)], out=attn[:, ch, t0:t0 + tw],
                       in0=attn[:, ch, t0:t0 + tw], scalar=vecs[:, 12 + ch:13 + ch], in1=nr[:, 1, 0:tw], op0=ALU.mult, op1=ALU.mult)
            wt2 = [A.alloc([8, 128], BF16) for _ in range(2)]
            wout_half(l, 1, attn, "attn", wt2)
            hv = A.alloc([NCH, NHALO], F32)
            ldf(hv, hv_d.rearrange("p (c n) -> p c n", n=NHALO), "hv")
            hk = [k for c in range(NCH) for k in xT_keys(c, C_H, NHALO)]
            op("dve", "tensor_tensor", hk + ["hv"], hk, out=xT[:, :, C_H:C_H + NHALO], in0=xT[:, :, C_H:C_H + NHALO], in1=hv, op=ALU.mult)

        for l in range(DEPTH):
            ffn(l, "ffn1")
            if stop_after != "nomix":
                mixer(l)
            ffn(l, "ffn2")

        for bi, (t0, tw) in enumerate(TB):
            si = next_stage()
            for c4 in range(NCH // 4):
                bank = c4 % 2
                for cc in range(4):
                    c = c4 * 4 + cc
                    P.op("pe", (lambda e, bank=bank, cc=cc, c=c, t0=t0, tw=tw: e.transpose(
                        out=psum[bank][0:tw, cc * 128:(cc + 1) * 128], in_=xT[:, c, t0:t0 + tw], identity=ident_f)),
                         xT_keys(c, t0, tw) + ["ident_f"], [PS(bank)])
                if c4 % 2 == 0:
                    P.op("dve", (lambda e, bank=bank, si=si, c4=c4, tw=tw: e.tensor_copy(
                        out=stage[si][0:tw, c4 * 512:(c4 + 1) * 512], in_=psum[bank][0:tw, :])), [PS(bank)], [("stage", si)])
                else:
                    P.op("act", (lambda e, bank=bank, si=si, c4=c4, tw=tw: e.activation(
                        out=stage[si][0:tw, c4 * 512:(c4 + 1) * 512], in_=psum[bank][0:tw, :], func=AF.Copy)), [PS(bank)], [("stage", si)])
            P.op("sp", (lambda e, si=si, t0=t0, tw=tw: e.dma_start(out=y_tok[t0:t0 + tw, :], in_=stage[si][0:tw, :])),
                 [("stage", si)], [("y", bi)], dma_key=("stage", si))
        P.barrier()
        nsem = P.emit(nc, lambda name: es.enter_context(nc.semaphore(name)))
    return nc, dict(NT=NT, nsem=nsem, peak=A.peak, ninstr={e: len(P.per_eng[e]) for e in ENGS})


_CACHE = {}


def _rope_tables(pos):
    inv = (np.float32(ROPE_THETA) ** (-np.arange(32, dtype=np.float32) / np.float32(32))).astype(np.float32)
    ang = pos.astype(np.float32)[:, None] * inv[None, :]
    return np.cos(ang).astype(np.float32), np.sin(ang).astype(np.float32)


def run_cfg(cfg, inputs):
    SPAN, NSEQ, NPAGES, DFF, NPOOLPG, DEPTH = (cfg[k] for k in ("SPAN", "NSEQ", "NPAGES", "DFF", "NPOOLPG", "DEPTH"))
    key = tuple(sorted(cfg.items()))
    if key not in _CACHE:
        _CACHE[key] = build(cfg)
    nc, info = _CACHE[key]
    NS = NSEQ * DEC_SEQ
    NR = N_META + SPAN
    C_P, C_H = N_META, N_META + SPAN
    C_S = C_H + NHALO
    NT = C_S + NS
    NQ = C_S
    PAST = NPAGES * PAGE
    f32 = np.float32
    g = {k: np.asarray(v) for k, v in inputs.items()}
    L = DEPTH

    def featT(v, nch):
        return np.ascontiguousarray(v.reshape(L, nch, 128).transpose(0, 2, 1)).astype(f32)

    shared = {"ident": np.eye(128, dtype=f32)}
    for nm in ("ffn1", "ffn2"):
        for s in ("_w_gate", "_w_up", "_w_down"):
            shared[nm + s] = g[nm + s]
        shared[nm + "_normT"] = featT(g[nm + "_norm"], NCH)
    shared["w_in"] = g["w_in"]
    shared["w_q_b"] = g["w_q_b"]
    shared["w_kv_b"] = g["w_kv_b"]
    shared["pool_w"] = g["pool_w"]
    shared["w_out"] = g["w_out"]
    shared["mix_normT"] = featT(g["mix_norm"], NCH)
    vecs = np.zeros((L, 128, 32), f32)
    vecs[:, :, 0:4] = featT(g["q_a_norm"], 4)
    vecs[:, :, 4:12] = featT(g["pool_out_norm"], 8)
    vecs[:, :, 12:20] = featT(g["attn_out_norm"], 8)
    vecs[:, :, 20:28] = featT(g["pool_scale"], 8)
    vecs[:, :, 28] = g["q_norm_nope"]
    vecs[:, :, 29] = g["k_norm_nope"]
    vecs[:, 0:64, 30] = np.concatenate([g["q_norm_rope"], g["q_norm_rope"]], -1)
    vecs[:, 0:64, 31] = np.concatenate([g["k_norm_rope"], g["k_norm_rope"]], -1)
    shared["vecs"] = vecs
    shared["kvnorm_b"] = np.ascontiguousarray(np.broadcast_to(g["kv_a_norm"][:, None, :], (L, 128, 256))).astype(f32)
    mk = np.zeros((128, 32), f32)
    for k in range(4):
        for h in range(8):
            for t in range(4):
                mk[k, h * 4 + t] = 1.0 if t >= k else 0.0
    shared["mknew"] = mk
    NKB_ = 1 + 4 * (SPAN // 128)
    kpos = np.zeros((128, NKB_), f32)
    kpos[:, 0] = np.arange(128)
    kpos[16:, 0] = 1e9
    for kb in range(1, NKB_):
        r, i = divmod(kb - 1, SPAN // 128)
        kpos[:, kb] = 16 + r * SPAN + i * 128 + np.arange(128)
    shared["kpos_t"] = kpos
    for ll in range(L):
        shared["cache_ckv%d" % ll] = g["cache_ckv"][ll].reshape(-1, 256)
        shared["cache_kpe%d" % ll] = g["cache_kpe"][ll].reshape(-1, 64)

    in_maps = []
    for core in range(8):
        b, j = divmod(core, 4)
        m = dict(shared)
        xp = g["x_prompt"][b]
        meta = g["meta_tokens"]
        if j == 0:
            halo = np.concatenate([np.zeros((14, D_MODEL), f32), meta], 0)
            hpos = np.concatenate([np.zeros(14), np.arange(16)])
            hval = np.concatenate([np.zeros(14), np.ones(16)])
        else:
            halo = xp[j * SPAN - NHALO:j * SPAN]
            hpos = 16 + j * SPAN - NHALO + np.arange(NHALO)
            hval = np.ones(NHALO)
        xs = g["x_sample"][core * NSEQ:(core + 1) * NSEQ].reshape(NS, D_MODEL)
        m["x_tok"] = np.ascontiguousarray(np.concatenate([meta, xp[j * SPAN:(j + 1) * SPAN], halo, xs], 0)).astype(f32)
        pos = np.concatenate([np.arange(16), 16 + j * SPAN + np.arange(SPAN), hpos, np.tile(PAST + np.arange(DEC_SEQ), NSEQ)]).astype(f32)
        m["qpos_b"] = np.ascontiguousarray(np.broadcast_to(pos[None, :NQ], (128, NQ))).astype(f32)
        m["hvtab"] = np.ascontiguousarray(np.broadcast_to(np.tile(hval, NCH)[None, :], (128, NCH * NHALO))).astype(f32)
        pmh = np.concatenate([np.arange(16), hpos])
        invc = np.stack([1.0 / np.minimum(pmh + 1, wd) for wd in POOL_WINDOWS], 0).reshape(-1)
        m["invcnt"] = np.ascontiguousarray(np.broadcast_to(invc[None, :], (128, 4 * 46))).astype(f32)
        c, s = _rope_tables(pos)
        m["cos_tok"], m["sin_tok"] = c, s
        m["cosT"] = np.ascontiguousarray(np.concatenate([c.T, c.T], 0))
        m["sinT"] = np.ascontiguousarray(np.concatenate([s.T, s.T], 0))
        m["state_pool"] = np.ascontiguousarray(g["state_pool"][:, core * NSEQ:(core + 1) * NSEQ].reshape(L, NSEQ * 15, POOL_WIDTH))
        m["page_table"] = np.ascontiguousarray(g["page_table"][core * NSEQ:(core + 1) * NSEQ].reshape(1, NSEQ * NPAGES)).astype(np.int32)
        in_maps.append(m)
    res = run_bass_kernel_spmd(nc, in_maps, core_ids=list(range(8)))
    R = res.results
    B = 2
    T = N_META + 4 * SPAN
    y_prompt = np.zeros((B, 4 * SPAN, D_MODEL), f32)
    y_sample = np.zeros((8 * NSEQ, DEC_SEQ, D_MODEL), f32)
    ckv_p = np.zeros((L, B, T, 256), f32)
    kpe_p = np.zeros((L, B, T, 64), f32)
    pool_p = np.zeros((L, B, 15, POOL_WIDTH), f32)
    ckv_s = np.zeros((L, 8 * NSEQ, DEC_SEQ, 256), f32)
    kpe_s = np.zeros((L, 8 * NSEQ, DEC_SEQ, 64), f32)
    pool_s = np.zeros((L, 8 * NSEQ, 15, POOL_WIDTH), f32)
    for core in range(8):
        b, j = divmod(core, 4)
        r = R[core]
        y_prompt[b, j * SPAN:(j + 1) * SPAN] = r["y_tok"][C_P:C_P + SPAN]
        y_sample[core * NSEQ:(core + 1) * NSEQ] = r["y_tok"][C_S:NT].reshape(NSEQ, DEC_SEQ, D_MODEL)
        ckv_p[:, b, 16 + j * SPAN:16 + (j + 1) * SPAN] = r["o_ckv_p"][:, 16:]
        kpe_p[:, b, 16 + j * SPAN:16 + (j + 1) * SPAN] = r["o_kpe_p"][:, 16:]
        if j == 0:
            ckv_p[:, b, 0:16] = r["o_ckv_p"][:, 0:16]
            kpe_p[:, b, 0:16] = r["o_kpe_p"][:, 0:16]
        if j == 3:
            pool_p[:, b] = r["o_pool_p"]
        ckv_s[:, core * NSEQ:(core + 1) * NSEQ] = r["o_ckv_s"].reshape(L, NSEQ, DEC_SEQ, 256)
        kpe_s[:, core * NSEQ:(core + 1) * NSEQ] = r["o_kpe_s"].reshape(L, NSEQ, DEC_SEQ, 64)
        pool_s[:, core * NSEQ:(core + 1) * NSEQ] = r["o_pool_s"].reshape(L, NSEQ, 15, POOL_WIDTH)
    return (y_prompt, y_sample, ckv_p, kpe_p, pool_p, ckv_s, kpe_s, pool_s)


def kernel(**inputs):
    return run_cfg(FULL_CFG, inputs)
```

```python
from contextlib import ExitStack
import numpy as np
import concourse.bass as bass
import concourse.mybir as mybir
from concourse.bass_utils import run_bass_kernel_spmd

F32 = mybir.dt.float32
BF16 = mybir.dt.bfloat16
I32 = mybir.dt.int32
AF = mybir.ActivationFunctionType
ALU = mybir.AluOpType
AX = mybir.AxisListType

D_MODEL = 2048
NCH = 16
N_META = 16
NHALO = 30
POOL_WIDTH = 1024
POOL_WINDOWS = (2, 4, 8, 16)
POOL_STATE = 15
N_HEADS = 8
NOPE = 128
ROPE = 64
QK_DIM = 192
V_DIM = 128
Q_LORA = 512
KV_LORA = 256
IN_WIDTH = POOL_WIDTH + Q_LORA + KV_LORA + ROPE
EPS = 1e-6
ROPE_THETA = 10000.0
ATTN_SCALE = QK_DIM ** -0.5
DEC_SEQ = 4
PAGE = 128

FULL_CFG = dict(SPAN=1024, NSEQ=16, NPAGES=64, DFF=5632, NPOOLPG=10240, DEPTH=2, BATCH=2)

ENGS = ("pe", "act", "dve", "pool", "sp")
SEM_LIMIT = 8000
DMA_LIMIT = 500


class Instr:
    __slots__ = ("eng", "fn", "waits", "signal", "idx", "dma_key", "vc", "clock", "gid")


class Prog:
    def __init__(self):
        self.per_eng = {e: [] for e in ENGS}
        self.last_w = {}
        self.readers = {}
        self.known = {e: {} for e in ENGS}
        self.dma_count = {}
        self.last_dma = {}
        self.gid = 0

    def _wait_on(self, eng, J, waits):
        known = self.known[eng]
        ck, cv = J.clock
        if J.eng == eng and J.dma_key is None and eng == "pe":
            return
        if known.get(ck, 0) >= cv:
            return
        waits.append(J)
        J.signal = True
        changed = dict(known)
        for k, v in J.vc.items():
            if changed.get(k, 0) < v:
                changed[k] = v
        if changed.get(ck, 0) < cv:
            changed[ck] = cv
        self.known[eng] = changed

    def op(self, eng, fn, reads=(), writes=(), dma_key=None):
        I = Instr()
        I.eng = eng
        I.fn = fn
        I.signal = False
        I.dma_key = dma_key
        I.gid = self.gid
        self.gid += 1
        deps = {}
        for r in reads:
            w = self.last_w.get(r)
            if w is not None:
                deps[w.gid] = w
        for w_ in writes:
            w = self.last_w.get(w_)
            if w is not None:
                deps[w.gid] = w
            for rd in self.readers.get(w_, ()):
                deps[rd.gid] = rd
        waits = []
        for g in sorted(deps, reverse=True):
            self._wait_on(eng, deps[g], waits)
        I.waits = waits
        lst = self.per_eng[eng]
        I.idx = len(lst)
        lst.append(I)
        if dma_key is None:
            I.clock = (eng, I.idx + 1)
        else:
            c = self.dma_count.get(dma_key, 0) + 1
            self.dma_count[dma_key] = c
            I.clock = (("dma", dma_key), c)
            self.last_dma[dma_key] = I
        I.vc = self.known[eng]
        for r in reads:
            self.readers.setdefault(r, []).append(I)
        for w_ in writes:
            self.last_w[w_] = I
            self.readers[w_] = []
        return I

    def barrier(self):
        lasts = []
        for e in ENGS:
            for J in reversed(self.per_eng[e]):
                if J.dma_key is None and J.fn is not None:
                    lasts.append(J)
                    break
        lasts += list(self.last_dma.values())
        for e in ENGS:
            I = Instr()
            I.eng = e
            I.fn = None
            I.signal = False
            I.dma_key = None
            I.gid = self.gid
            self.gid += 1
            waits = []
            for J in sorted(lasts, key=lambda j: -j.gid):
                self._wait_on(e, J, waits)
            I.waits = waits
            lst = self.per_eng[e]
            I.idx = len(lst)
            lst.append(I)
            I.clock = (e, I.idx + 1)
            I.vc = self.known[e]

    def emit(self, nc, new_sem):
        sig_of = {}
        for e in ENGS:
            cnt = 0
            semi = 0
            for I in self.per_eng[e]:
                if I.dma_key is not None or I.fn is None:
                    continue
                if I.signal:
                    if cnt >= SEM_LIMIT:
                        semi += 1
                        cnt = 0
                    cnt += 1
                    sig_of[I.gid] = ((e, semi), cnt)
        sems = {}

        def get_sem(key):
            if key not in sems:
                sems[key] = new_sem("s%d" % len(sems))
            return sems[key]

        def emit_eng(e, engobj):
            for I in self.per_eng[e]:
                for J in I.waits:
                    if J.dma_key is not None:
                        inc = DMA_INC.get(J.dma_key, 16)
                        ep, cnt = divmod(J.clock[1] - 1, DMA_LIMIT)
                        if ep > 0:
                            engobj.wait_ge(get_sem(("dma", J.dma_key, ep - 1)), inc * DMA_LIMIT)
                        engobj.wait_ge(get_sem(("dma", J.dma_key, ep)), inc * (cnt + 1))
                    else:
                        sk, cv = sig_of[J.gid]
                        engobj.wait_ge(get_sem(sk), cv)
                if I.fn is None:
                    continue
                ins = I.fn(engobj)
                if I.dma_key is not None:
                    ins.then_inc(get_sem(("dma", I.dma_key, (I.clock[1] - 1) // DMA_LIMIT)), DMA_INC.get(I.dma_key, 16))
                elif I.signal:
                    sk, cv = sig_of[I.gid]
                    ins.then_inc(get_sem(sk), 1)

        with nc.Block() as block:
            @block.tensor
            def _(eng):
                emit_eng("pe", eng)

            @block.scalar
            def _(eng):
                emit_eng("act", eng)

            @block.vector
            def _(eng):
                emit_eng("dve", eng)

            @block.gpsimd
            def _(eng):
                emit_eng("pool", eng)

            @block.sync
            def _(eng):
                emit_eng("sp", eng)
        return len(sems)


DMA_INC = {}


class Arena:
    def __init__(self, t, words, base=0):
        self.t = t
        self.base = base
        self.words = base + words
        self.off = base
        self.peak = base

    def mark(self):
        return self.off

    def reset(self, m):
        self.off = m

    def room(self, shape, dtype):
        n = 1
        for s in shape:
            n *= s
        esz = 2 if dtype == BF16 else 4
        words = ((n * esz + 3) // 4 + 7) // 8 * 8
        return self.off + words <= self.words

    def alloc(self, shape, dtype, parts=128):
        n = 1
        for s in shape:
            n *= s
        esz = 2 if dtype == BF16 else 4
        words = (n * esz + 3) // 4
        words = (words + 7) // 8 * 8
        assert self.off + words <= self.words, ("SBUF arena overflow", self.off, words, self.words)
        ap = self.t[0:parts, self.off:self.off + words]
        self.off += words
        self.peak = max(self.peak, self.off)
        if dtype != F32:
            ap = ap.bitcast(dtype)
        ap = ap[:, 0:n]
        if len(shape) == 2:
            return ap.rearrange("p (a b) -> p a b", b=shape[1])
        if len(shape) == 3:
            return ap.rearrange("p (a b c) -> p a b c", b=shape[1], c=shape[2])
        return ap


def tiles_of(n, maxw):
    k = -(-n // maxw)
    base = -(-n // k)
    out = []
    s = 0
    while s < n:
        w = min(base, n - s)
        out.append((s, w))
        s += w
    return out


def build(cfg, stop_after=None):
    SPAN, NSEQ, NPAGES, DFF, NPOOLPG, DEPTH = (cfg[k] for k in ("SPAN", "NSEQ", "NPAGES", "DFF", "NPOOLPG", "DEPTH"))
    NS = NSEQ * DEC_SEQ
    C_M, C_P, C_H = 0, N_META, N_META + SPAN
    C_S = C_H + NHALO
    NT = C_S + NS
    NQ = C_S
    NR = N_META + SPAN
    NCHK = max(1, SPAN // 256)
    CP = SPAN // NCHK
    CH_A = [0 if q == 0 else N_META + q * CP for q in range(NCHK)]
    CH_B = [N_META + (q + 1) * CP for q in range(NCHK)]
    CH_R = [CH_B[q] - CH_A[q] for q in range(NCHK)]
    KBS = [(0, 0, 16)]
    for r in range(4):
        for i in range(SPAN // 128):
            q = (i * 128) // CP
            KBS.append((q, r * CH_R[q] + (N_META + i * 128 - CH_A[q]), 128))
    NKB = len(KBS)
    U8 = mybir.dt.uint8
    NFC = DFF // 128
    GS = 4
    assert NFC % GS == 0
    NG = NFC // GS
    TT = tiles_of(NT, 512)
    TB = tiles_of(NT, 128)
    TB = [(s, min(128, NT - s)) for s in range(0, NT, 128)]

    nc = bass.Bass("TRN2", target_bir_lowering=False)

    def din(name, shape, dt=F32):
        return nc.dram_tensor(name, list(shape), dt, kind="ExternalInput").ap()

    def dout(name, shape, dt=F32):
        return nc.dram_tensor(name, list(shape), dt, kind="ExternalOutput").ap()

    x_tok = din("x_tok", [NT, D_MODEL])
    ident_d = din("ident", [128, 128])
    w = {}
    for nm in ("ffn1", "ffn2"):
        w[nm + "_w_gate"] = din(nm + "_w_gate", [DEPTH, D_MODEL, DFF])
        w[nm + "_w_up"] = din(nm + "_w_up", [DEPTH, D_MODEL, DFF])
        w[nm + "_w_down"] = din(nm + "_w_down", [DEPTH, DFF, D_MODEL])
        w[nm + "_normT"] = din(nm + "_normT", [DEPTH, 128, NCH])
    y_tok = dout("y_tok", [NT, D_MODEL])
    w_in_d = din("w_in", [DEPTH, D_MODEL, IN_WIDTH])
    w_qb_d = din("w_q_b", [DEPTH, Q_LORA, N_HEADS * QK_DIM])
    w_kvb_d = din("w_kv_b", [DEPTH, KV_LORA, N_HEADS * 256])
    pool_w_d = din("pool_w", [DEPTH, 4, 256, 256])
    w_out_d = din("w_out", [DEPTH, D_MODEL, D_MODEL])
    mix_normT_d = din("mix_normT", [DEPTH, 128, NCH])
    vecs_d = din("vecs", [DEPTH, 128, 32])
    kvnorm_d = din("kvnorm_b", [DEPTH, 128, 256])
    qpos_d = din("qpos_b", [128, NQ])
    hv_d = din("hvtab", [128, NCH * NHALO])
    invc_d = din("invcnt", [128, 4 * 46])
    mknew_d = din("mknew", [128, 32])
    kpos_d = din("kpos_t", [128, NKB])
    cosT_d = din("cosT", [64, NT])
    sinT_d = din("sinT", [64, NT])
    cost_d = din("cos_tok", [NT, 32])
    sint_d = din("sin_tok", [NT, 32])
    cckv_d = [din("cache_ckv%d" % l, [NPOOLPG * 128, 256]).rearrange("n d -> (n d)") for l in range(DEPTH)]
    ckpe_d = [din("cache_kpe%d" % l, [NPOOLPG * 128, 64]).rearrange("n d -> (n d)") for l in range(DEPTH)]
    state_d = din("state_pool", [DEPTH, NSEQ * 15, POOL_WIDTH])
    pt_d = din("page_table", [1, NSEQ * NPAGES], I32)
    o_ckv_p = dout("o_ckv_p", [DEPTH, NR, 256])
    o_kpe_p = dout("o_kpe_p", [DEPTH, NR, 64])
    o_pool_p = dout("o_pool_p", [DEPTH, 15, POOL_WIDTH])
    o_ckv_s = dout("o_ckv_s", [DEPTH, NS, 256])
    o_kpe_s = dout("o_kpe_s", [DEPTH, NS, 64])
    o_pool_s = dout("o_pool_s", [DEPTH, NSEQ * 15, POOL_WIDTH])
    cc_in = [[nc.dram_tensor("cc_in%d_%d" % (l, q), [CH_R[q], 320], F32, kind="Internal").ap() for q in range(NCHK)] for l in range(DEPTH)]
    cc_out = [[nc.dram_tensor("cc_out%d_%d" % (l, q), [4 * CH_R[q], 320], F32, kind="Internal", addr_space="Local").ap() for q in range(NCHK)]
              for l in range(DEPTH)]
    kvS = [nc.dram_tensor("kvS%d" % l, [NS, 320], F32, kind="Internal").ap() for l in range(DEPTH)]
    ctok_d = [nc.dram_tensor("ctokd%d" % l, [NKB * 128, 256], BF16, kind="Internal").ap() for l in range(DEPTH)]

    P = Prog()
    es = ExitStack()
    with es:
        WORDS = 53100
        arena_t = es.enter_context(nc.sbuf_tensor("arena", [128, WORDS], F32))
        A = Arena(arena_t, WORDS)
        psum = [es.enter_context(nc.psum_tensor("ps%d" % i, [128, 512], F32)) for i in range(8)]

        def PS(i):
            return ("ps", i)

        xT = A.alloc([NCH, NT], F32)
        xn_off = A.off
        xnT = A.alloc([NCH, NT], BF16)
        xn_words = A.off - xn_off
        ident_f = A.alloc([1, 128], F32)[:, 0, :]
        ident_b = A.alloc([1, 128], BF16)[:, 0, :]
        ones_b = A.alloc([1, 128], BF16)[:, 0, :]
        eps_t = A.alloc([1, 8], F32)[:, 0, :]
        NSTG = 3
        st_off = A.off
        stage = [A.alloc([1, 2048], F32)[:, 0, :] for _ in range(NSTG)]
        st_i = [0]

        def next_stage():
            i = st_i[0] % NSTG
            st_i[0] += 1
            return i

        cast_rr = [0]
        REGS = {}

        def cast_eng():
            e = ("act", "pool", "dve")[cast_rr[0] % 3]
            cast_rr[0] += 1
            return e

        def cast(eng, out, in_, reads, writes):
            if eng == "act":
                P.op("act", lambda e: e.activation(out=out, in_=in_, func=AF.Copy), reads, writes)
            else:
                P.op(eng, lambda e: e.tensor_copy(out=out, in_=in_), reads, writes)

        P.op("sp", lambda e: e.dma_start(out=ident_f, in_=ident_d), [], ["ident_f"], dma_key="const")
        P.op("dve", lambda e: e.tensor_copy(out=ident_b, in_=ident_f), ["ident_f"], ["ident_b"])
        P.op("dve", lambda e: e.memset(ones_b, 1.0), [], ["ones_b"])
        P.op("dve", lambda e: e.memset(eps_t, EPS), [], ["eps"])

        for bi, (t0, tw) in enumerate(TB):
            si = next_stage()
            P.op("sp", (lambda e, si=si, t0=t0, tw=tw: e.dma_start(out=stage[si][0:tw, :], in_=x_tok[t0:t0 + tw, :])),
                 [], [("stage", si)], dma_key=("stage", si))
            for c4 in range(NCH // 4):
                bank = c4 % 2
                for cc in range(4):
                    c = c4 * 4 + cc
                    P.op("pe", (lambda e, bank=bank, cc=cc, c=c, si=si, tw=tw: e.transpose(
                        out=psum[bank][:, cc * 128:cc * 128 + tw], in_=stage[si][0:tw, c * 128:(c + 1) * 128],
                        identity=ident_f[0:tw, 0:tw])), [("stage", si), "ident_f"], [PS(bank)])
                eng = "dve" if c4 % 2 == 0 else "act"
                src = (lambda bank=bank, tw=tw: psum[bank][:, :].rearrange("p (a b) -> p a b", b=128)[:, :, 0:tw])
                dst = (lambda c4=c4, t0=t0, tw=tw: xT[:, c4 * 4:c4 * 4 + 4, t0:t0 + tw])
                if eng == "dve":
                    P.op("dve", (lambda e, src=src, dst=dst: e.tensor_copy(out=dst(), in_=src())),
                         [PS(bank)], [("xT", c, bi) for c in range(c4 * 4, c4 * 4 + 4)])
                else:
                    P.op("act", (lambda e, src=src, dst=dst: e.activation(out=dst(), in_=src(), func=AF.Copy)),
                         [PS(bank)], [("xT", c, bi) for c in range(c4 * 4, c4 * 4 + 4)])

        def xT_keys(c, t0, tw):
            b0, b1 = t0 // 128, (t0 + tw - 1) // 128
            return [("xT", c, b) for b in range(b0, b1 + 1)]

        def all_xT_keys(t0, tw):
            return [k for c in range(NCH) for k in xT_keys(c, t0, tw)]

        def rmsnorm_full(gT_dram):
            m = A.mark()
            gT = A.alloc([1, NCH], F32)[:, 0, :]
            P.op("sp", lambda e: e.dma_start(out=gT, in_=gT_dram), [], ["gT"], dma_key="gT")
            sq = A.alloc([NCH, 512], BF16)
            tmp = A.alloc([1, 512], F32)[:, 0, :]
            rstd = A.alloc([1, 512], F32)[:, 0, :]
            for ti, (t0, tw) in enumerate(TT):
                P.op("act", (lambda e, t0=t0, tw=tw: e.activation(out=sq[:, :, 0:tw], in_=xT[:, :, t0:t0 + tw], func=AF.Square)),
                     all_xT_keys(t0, tw), ["n_sq"])
                for c in range(NCH):
                    P.op("pe", (lambda e, c=c, tw=tw: e.matmul(psum[7][:, 0:tw], lhsT=ones_b, rhs=sq[:, c, 0:tw],
                                                             start=(c == 0), stop=(c == NCH - 1))),
                         ["n_sq", "ones_b"], [PS(7)])
                P.op("act", (lambda e, tw=tw: e.activation(out=tmp[:, 0:tw], in_=psum[7][:, 0:tw], func=AF.Sqrt,
                                                          bias=eps_t[:, 0:1], scale=1.0 / D_MODEL)),
                     [PS(7), "eps"], ["n_tmp"])
                P.op("dve", (lambda e, tw=tw: e.reciprocal(out=rstd[:, 0:tw], in_=tmp[:, 0:tw])), ["n_tmp"], ["n_rstd"])
                for c in range(NCH):
                    P.op("dve", (lambda e, c=c, t0=t0, tw=tw: e.scalar_tensor_tensor(
                        out=xnT[:, c, t0:t0 + tw], in0=xT[:, c, t0:t0 + tw], scalar=gT[:, c:c + 1], in1=rstd[:, 0:tw],
                        op0=ALU.mult, op1=ALU.mult)),
                         xT_keys(c, t0, tw) + ["gT", "n_rstd"], [("xnT", c, ti)])
            P.barrier()
            A.reset(m)

        def ffn(l, nm):
            wg_d = w[nm + "_w_gate"][l].rearrange("(c p) f -> p c f", p=128)
            wu_d = w[nm + "_w_up"][l].rearrange("(c p) f -> p c f", p=128)
            wd_d = w[nm + "_w_down"][l]
            rmsnorm_full(w[nm + "_normT"][l])
            m = A.mark()
            NWB = 2
            wg_b = [A.alloc([NCH, 128], BF16) for _ in range(NWB)]
            wu_b = [A.alloc([NCH, 128], BF16) for _ in range(NWB)]
            NWD = 6
            wd_b = [A.alloc([1, D_MODEL], BF16)[:, 0, :] for _ in range(NWD)]
            aT = [A.alloc([GS, NT], BF16) for _ in range(2)]
            sil = [A.alloc([1, 512], F32)[:, 0, :] for _ in range(2)]
            cnt = dict(wb=0, wd=0, sil=0, gu=0, dn=0)

            def up_phase(g):
                ab = g % 2
                for jj in range(GS):
                    j = g * GS + jj
                    wb = cnt["wb"] % NWB
                    cnt["wb"] += 1
                    for (wd_, wb_t, nmk) in ((wg_d, wg_b, "wg"), (wu_d, wu_b, "wu")):
                        si = next_stage()
                        P.op("sp", (lambda e, si=si, wd_=wd_, j=j: e.dma_start(
                            out=stage[si].rearrange("p (c f) -> p c f", f=128), in_=wd_[:, :, j * 128:(j + 1) * 128])),
                             [], [("stage", si)], dma_key=("stage", si))
                        cast(cast_eng(), wb_t[wb].rearrange("p c f -> p (c f)"), stage[si], [("stage", si)], [(nmk, wb)])
                    for ti, (t0, tw) in enumerate(TT):
                        bg = (cnt["gu"] % 2) * 2
                        cnt["gu"] += 1
                        for (wb_t, nmk, bank) in ((wg_b, "wg", bg), (wu_b, "wu", bg + 1)):
                            for c in range(NCH):
                                P.op("pe", (lambda e, wb_t=wb_t, wb=wb, c=c, t0=t0, tw=tw, bank=bank: e.matmul(
                                    psum[bank][:, 0:tw], lhsT=wb_t[wb][:, c, :], rhs=xnT[:, c, t0:t0 + tw],
                                    start=(c == 0), stop=(c == NCH - 1))),
                                     [(nmk, wb), ("xnT", c, ti)], [PS(bank)])
                        sb = cnt["sil"] % 2
                        cnt["sil"] += 1
                        P.op("act", (lambda e, sb=sb, tw=tw, bg=bg: e.activation(out=sil[sb][:, 0:tw], in_=psum[bg][:, 0:tw], func=AF.Silu)),
                             [PS(bg)], [("sil", sb)])
                        P.op("dve", (lambda e, sb=sb, tw=tw, bg=bg, ab=ab, jj=jj, t0=t0: e.tensor_tensor(
                            out=aT[ab][:, jj, t0:t0 + tw], in0=sil[sb][:, 0:tw], in1=psum[bg + 1][:, 0:tw], op=ALU.mult)),
                             [("sil", sb), PS(bg + 1)], [("aT", ab, jj, ti)])

            def down_phase(g):
                ab = g % 2
                slots = []
                for jj in range(GS):
                    j = g * GS + jj
                    ws = cnt["wd"] % NWD
                    cnt["wd"] += 1
                    slots.append(ws)
                    si = next_stage()
                    P.op("sp", (lambda e, si=si, j=j: e.dma_start(out=stage[si], in_=wd_d[j * 128:(j + 1) * 128, :])),
                         [], [("stage", si)], dma_key=("stage", si))
                    cast(cast_eng(), wd_b[ws], stage[si], [("stage", si)], [("wd", ws)])
                for i in range(NCH):
                    for ti, (t0, tw) in enumerate(TT):
                        bank = 4 + cnt["dn"] % 2
                        cnt["dn"] += 1
                        for jj in range(GS):
                            P.op("pe", (lambda e, bank=bank, jj=jj, i=i, t0=t0, tw=tw, ws=slots[jj]: e.matmul(
                                psum[bank][:, 0:tw], lhsT=wd_b[ws][:, i * 128:(i + 1) * 128], rhs=aT[ab][:, jj, t0:t0 + tw],
                                start=(jj == 0), stop=(jj == GS - 1))),
                                 [("wd", slots[jj]), ("aT", ab, jj, ti)], [PS(bank)])
                        P.op("dve", (lambda e, bank=bank, i=i, t0=t0, tw=tw: e.scalar_tensor_tensor(
                            out=xT[:, i, t0:t0 + tw], in0=psum[bank][:, 0:tw], scalar=0.5, in1=xT[:, i, t0:t0 + tw],
                            op0=ALU.mult, op1=ALU.add)),
                             [PS(bank)] + xT_keys(i, t0, tw), xT_keys(i, t0, tw))

            up_phase(0)
            for g in range(NG):
                if g + 1 < NG:
                    up_phase(g + 1)
                down_phase(g)
            P.barrier()
            A.reset(m)

        def op(eng, method, reads, writes, dma_key=None, **kw):
            return P.op(eng, (lambda e: getattr(e, method)(**kw)), reads, writes, dma_key=dma_key)

        def xn_keys(c, t0, tw):
            return [("xnT", c, ti) for ti, (a, b) in enumerate(TT) if a < t0 + tw and t0 < a + b]

        def seg_iter(t0, tw):
            for nm, a, b in (("M", C_M, C_P), ("P", C_P, C_H), ("H", C_H, C_S), ("S", C_S, NT)):
                lo, hi = max(t0, a), min(t0 + tw, b)
                if lo < hi:
                    yield nm, lo - a, lo - t0, hi - lo

        def ld(dst, src, n, key, f=None):
            si = next_stage()
            sv = stage[si][:, 0:n]
            if f is not None:
                sv = sv.rearrange("p (c f) -> p c f", f=f)
            op("sp", "dma_start", [], [("stage", si)], dma_key=("stage", si), out=sv, in_=src)
            cast(cast_eng(), dst, sv, [("stage", si)], [key])

        def ldf(dst, src, key, eng="sp"):
            op(eng, "dma_start", [], [key], dma_key=("ldf", key), out=dst, in_=src)

        def mixer(l):
            m_mix = A.mark()
            w_in_v = w_in_d[l].rearrange("(c p) f -> p c f", p=128)
            rmsnorm_full(mix_normT_d[l])
            vecs = A.alloc([1, 32], F32)[:, 0, :]
            ldf(vecs, vecs_d[l], "vecs")
            kvn = A.alloc([1, 256], F32)[:, 0, :]
            ldf(kvn, kvnorm_d[l], "kvn")
            b192 = A.alloc([1, 8], F32)[:, 0, :]
            op("dve", "memset", [], ["b192"], ap=b192, constant=192.0 * EPS)
            gqk = A.alloc([1, 8], F32)[:, 0, :]
            op("dve", "tensor_tensor", ["vecs"], ["gqk"], out=gqk[:, 0:1], in0=vecs[:, 28:29], in1=vecs[:, 29:30], op=ALU.mult)
            op("dve", "tensor_tensor", ["vecs", "gqk"], ["gqk"], out=gqk[:, 1:2], in0=vecs[:, 30:31], in1=vecs[:, 31:32], op=ALU.mult)

            m_b = A.mark()
            wkvin = A.alloc([NCH, 320], BF16)
            for q4 in range(4):
                ld(wkvin[:, q4 * 4:(q4 + 1) * 4, :], w_in_v[:, q4 * 4:(q4 + 1) * 4, 1536:1856], 1280, ("wkvin", q4), f=320)
            kvo = [A.alloc([1, 320], F32)[:, 0, :] for _ in range(2)]
            junk = A.alloc([1, 256], F32)[:, 0, :]
            sm = A.alloc([1, 8], F32)[:, 0, :]
            ct = [A.alloc([1, 32], F32)[:, 0, :] for _ in range(2)]
            st_ = [A.alloc([1, 32], F32)[:, 0, :] for _ in range(2)]
            rt = A.alloc([4, 32], F32)
            ccin_keys = []
            for bi, (t0, tw) in enumerate(TB):
                kb_ = bi % 2
                ps = psum[bi % 2]
                op("sp", "dma_start", [], [("ct", kb_)], dma_key=("ct", kb_), out=ct[kb_][0:tw, :], in_=cost_d[t0:t0 + tw, :])
                op("sp", "dma_start", [], [("st", kb_)], dma_key=("st", kb_), out=st_[kb_][0:tw, :], in_=sint_d[t0:t0 + tw, :])
                for c in range(NCH):
                    op("pe", "matmul", xn_keys(c, t0, tw) + [("wkvin", c // 4)], [PS(bi % 2)], out=ps[0:tw, 0:320],
                       lhsT=xnT[:, c, t0:t0 + tw], rhs=wkvin[:, c, :], start=(c == 0), stop=(c == NCH - 1))
                op("act", "activation", [PS(bi % 2)], ["junk"], out=junk[0:tw, :], in_=ps[0:tw, 0:256], func=AF.Square)
                op("dve", "tensor_reduce", ["junk"], ["sm0"], out=sm[0:tw, 0:1], in_=junk[0:tw, :], axis=AX.X, op=ALU.add)
                op("act", "activation", ["sm0", "eps"], ["sm1"], out=sm[0:tw, 1:2], in_=sm[0:tw, 0:1], func=AF.Sqrt, bias=eps_t[0:tw, 0:1], scale=1.0 / 256)
                op("dve", "reciprocal", ["sm1"], ["sm2"], out=sm[0:tw, 2:3], in_=sm[0:tw, 1:2])
                ko = kvo[kb_]
                op("dve", "scalar_tensor_tensor", [PS(bi % 2), "sm2", "kvn"], [("kvo", kb_)], out=ko[0:tw, 0:256], in0=ps[0:tw, 0:256],
                   scalar=sm[0:tw, 2:3], in1=kvn[0:tw, :], op0=ALU.mult, op1=ALU.mult)
                x1, x2 = ps[0:tw, 256:288], ps[0:tw, 288:320]
                op("dve", "tensor_tensor", [PS(bi % 2), ("ct", kb_)], ["rt0"], out=rt[0:tw, 0, :], in0=x1, in1=ct[kb_][0:tw, :], op=ALU.mult)
                op("dve", "tensor_tensor", [PS(bi % 2), ("st", kb_)], ["rt1"], out=rt[0:tw, 1, :], in0=x2, in1=st_[kb_][0:tw, :], op=ALU.mult)
                op("dve", "tensor_tensor", [PS(bi % 2), ("st", kb_)], ["rt2"], out=rt[0:tw, 2, :], in0=x1, in1=st_[kb_][0:tw, :], op=ALU.mult)
                op("dve", "tensor_tensor", [PS(bi % 2), ("ct", kb_)], ["rt3"], out=rt[0:tw, 3, :], in0=x2, in1=ct[kb_][0:tw, :], op=ALU.mult)
                op("dve", "tensor_tensor", ["rt0", "rt1"], [("kvo", kb_)], out=ko[0:tw, 256:288], in0=rt[0:tw, 0, :], in1=rt[0:tw, 1, :], op=ALU.subtract)
                op("dve", "tensor_tensor", ["rt2", "rt3"], [("kvo", kb_)], out=ko[0:tw, 288:320], in0=rt[0:tw, 2, :], in1=rt[0:tw, 3, :], op=ALU.add)
                for nm, so, src, n in seg_iter(t0, tw):
                    if nm == "H":
                        continue
                    dk = ("kvo", kb_)
                    if nm in ("M", "P"):
                        ro = so + (0 if nm == "M" else N_META)
                        op("sp", "dma_start", [dk], [("o_ckv_p", l, bi, nm)], dma_key=dk, out=o_ckv_p[l][ro:ro + n, :], in_=ko[src:src + n, 0:256])
                        op("sp", "dma_start", [dk], [("o_kpe_p", l, bi, nm)], dma_key=dk, out=o_kpe_p[l][ro:ro + n, :], in_=ko[src:src + n, 256:320])
                        for q in range(NCHK):
                            lo, hi = max(ro, CH_A[q]), min(ro + n, CH_B[q])
                            if lo < hi:
                                kk = ("ccin", l, bi, nm, q)
                                op("sp", "dma_start", [dk], [kk], dma_key=dk, out=cc_in[l][q][lo - CH_A[q]:hi - CH_A[q], :],
                                   in_=ko[src + lo - ro:src + hi - ro, :])
                                ccin_keys.append((q, kk))
                    else:
                        op("sp", "dma_start", [dk], [("o_ckv_s", l, bi)], dma_key=dk, out=o_ckv_s[l][so:so + n, :], in_=ko[src:src + n, 0:256])
                        op("sp", "dma_start", [dk], [("o_kpe_s", l, bi)], dma_key=dk, out=o_kpe_s[l][so:so + n, :], in_=ko[src:src + n, 256:320])
                        op("sp", "dma_start", [dk], [("kvS", l)], dma_key=dk, out=kvS[l][so:so + n, :], in_=ko[src:src + n, :])
            for q in range(NCHK):
                DMA_INC[("cc", l, q)] = 1
                op("pool", "collective_compute", [kk for (qq, kk) in ccin_keys if qq == q], [("ccout", l, q)], dma_key=("cc", l, q), kind="AllGather",
                   op=ALU.bypass, replica_groups=[[0, 1, 2, 3], [4, 5, 6, 7]], ins=[cc_in[l][q]], outs=[cc_out[l][q]])
            P.barrier()
            A.reset(m_b)

            qlat = A.alloc([4, NT], BF16)
            m_d = A.mark()
            winq = A.alloc([NCH, 512], BF16)
            for mt in range(4):
                ld(winq[:, :, mt * 128:(mt + 1) * 128], w_in_v[:, :, 1024 + mt * 128:1024 + (mt + 1) * 128], 2048, ("winq", mt), f=128)
            qraw = A.alloc([4, 512], BF16)
            sq4 = A.alloc([4, 512], BF16)
            tmpf = A.alloc([2, 512], F32)
            for ti, (t0, tw) in enumerate(TT):
                for mt in range(4):
                    bank = mt % 2
                    for c in range(NCH):
                        op("pe", "matmul", [("winq", mt), ("xnT", c, ti)], [PS(bank)], out=psum[bank][:, 0:tw],
                           lhsT=winq[:, c, mt * 128:(mt + 1) * 128], rhs=xnT[:, c, t0:t0 + tw], start=(c == 0), stop=(c == NCH - 1))
                    op("act", "activation", [PS(bank)], [("qraw", mt)], out=qraw[:, mt, 0:tw], in_=psum[bank][:, 0:tw], func=AF.Copy)
                    op("act", "activation", [PS(bank)], [("sq4", mt)], out=sq4[:, mt, 0:tw], in_=psum[bank][:, 0:tw], func=AF.Square)
                for mt in range(4):
                    op("pe", "matmul", [("sq4", mt), "ones_b"], [PS(7)], out=psum[7][:, 0:tw], lhsT=ones_b, rhs=sq4[:, mt, 0:tw],
                       start=(mt == 0), stop=(mt == 3))
                op("act", "activation", [PS(7), "eps"], ["tmpf0"], out=tmpf[:, 0, 0:tw], in_=psum[7][:, 0:tw], func=AF.Sqrt,
                   bias=eps_t[:, 0:1], scale=1.0 / Q_LORA)
                op("dve", "reciprocal", ["tmpf0"], ["tmpf1"], out=tmpf[:, 1, 0:tw], in_=tmpf[:, 0, 0:tw])
                for mt in range(4):
                    op("dve", "scalar_tensor_tensor", [("qraw", mt), "vecs", "tmpf1"], [("qlat", mt, ti)], out=qlat[:, mt, t0:t0 + tw],
                       in0=qraw[:, mt, 0:tw], scalar=vecs[:, mt:mt + 1], in1=tmpf[:, 1, 0:tw], op0=ALU.mult, op1=ALU.mult)
            P.barrier()
            A.reset(m_d)

            m_e = A.mark()
            UW = 76 + SPAN
            UZ, UM, UH, UP = 0, 15, 46, 76
            poolraw = A.alloc([8, NT], BF16)
            poolw = A.alloc([8, 256], BF16)
            ld(poolw, pool_w_d[l].rearrange("g (cc p) d -> p (g cc) d", p=128), 2048, "poolw", f=256)
            invc = A.alloc([4, 46], F32)
            ldf(invc, invc_d.rearrange("p (g n) -> p g n", n=46), "invc")
            ub = [A.alloc([1, UW], F32)[:, 0, :] for _ in range(3)]
            us = [A.alloc([NSEQ, 19], F32) for _ in range(3)]
            dT = A.alloc([2, NT], BF16)
            wt = [A.alloc([NCH, 128], BF16) for _ in range(2)]
            t46 = A.alloc([1, 46], F32)[:, 0, :]
            NSH = -(-NSEQ // 8)
            sstt = [A.alloc([1, 128], F32)[:, 0, :] for _ in range(2)]
            opt = [A.alloc([1, 128], F32)[:, 0, :] for _ in range(2)]
            ost = [A.alloc([1, 128], F32)[:, 0, :] for _ in range(2)]
            rot = dict(s=0, p=0, o=0)
            for u3 in ub:
                op("dve", "memset", [], ["ub0", "ub1", "ub2"], ap=u3[:, 0:UH], constant=0.0)
            for g in range(4):
                wnd = POOL_WINDOWS[g]
                for cc in range(2):
                    ch = g * 2 + cc
                    wb = ch % 2
                    ld(wt[wb], w_in_v[:, :, ch * 128:(ch + 1) * 128], 2048, ("wt", wb), f=128)
                    u0, u1, u2 = ub
                    s0, s1, s2 = us
                    for hf in range(NSH):
                        ns = min(8, NSEQ - hf * 8)
                        sb = rot["s"] % 2
                        rot["s"] += 1
                        op("sp", "dma_start", [], [("sstt", sb)], dma_key=("sstt", sb), out=sstt[sb][0:ns * 15, :],
                           in_=state_d[l][hf * 120:hf * 120 + ns * 15, ch * 128:(ch + 1) * 128])
                        op("pe", "transpose", [("sstt", sb), "ident_f"], [PS(6)], out=psum[6][:, 0:ns * 15],
                           in_=sstt[sb][0:ns * 15, :], identity=ident_f[0:ns * 15, 0:ns * 15])
                        op("dve", "tensor_copy", [PS(6)], ["us0"], out=s0[:, hf * 8:hf * 8 + ns, 0:15],
                           in_=psum[6][:, 0:ns * 15].rearrange("p (s k) -> p s k", k=15))
                    for ti, (t0, tw) in enumerate(TT):
                        bank = ti % 2
                        for c in range(NCH):
                            op("pe", "matmul", [("wt", wb), ("xnT", c, ti)], [PS(bank)], out=psum[bank][:, 0:tw],
                               lhsT=wt[wb][:, c, :], rhs=xnT[:, c, t0:t0 + tw], start=(c == 0), stop=(c == NCH - 1))
                        for nm, so, src, n in seg_iter(t0, tw):
                            if nm == "S":
                                s_a, s_b = so // 4, (so + n) // 4
                                op("act", "activation", [PS(bank)], ["us0"], out=s0[:, s_a:s_b, 15:19],
                                   in_=psum[bank][:, src:src + n].rearrange("p (s k) -> p s k", k=4), func=AF.Copy)
                            else:
                                uo = {"M": UM, "H": UH, "P": UP}[nm] + so
                                op("act", "activation", [PS(bank)], ["ub0"], out=u0[:, uo:uo + n], in_=psum[bank][:, src:src + n], func=AF.Copy)
                    pb_ = rot["p"] % 2
                    rot["p"] += 1
                    op("pe", "transpose", ["ub0", "ident_f"], [PS(6)], out=psum[6][0:15, 0:128], in_=u0[:, UW - 15:UW], identity=ident_f)
                    op("dve", "tensor_copy", [PS(6)], [("opt", pb_)], out=opt[pb_][0:15, :], in_=psum[6][0:15, 0:128])
                    op("sp", "dma_start", [("opt", pb_)], [("o_pool_p", l, ch)], dma_key=("opt", pb_), out=o_pool_p[l][:, ch * 128:(ch + 1) * 128], in_=opt[pb_][0:15, :])
                    for hf in range(NSH):
                        ns = min(8, NSEQ - hf * 8)
                        ob = rot["o"] % 2
                        rot["o"] += 1
                        op("dve", "tensor_copy", ["us0", "ub2"], ["ub2"], out=ub[2][:, 0:ns * 15].rearrange("p (s k) -> p s k", k=15), in_=s0[:, hf * 8:hf * 8 + ns, 4:19])
                        op("pe", "transpose", ["ub2", "ident_f"], [PS(6)], out=psum[6][0:ns * 15, 0:128], in_=ub[2][:, 0:ns * 15], identity=ident_f)
                        op("dve", "tensor_copy", [PS(6)], [("ost", ob)], out=ost[ob][0:ns * 15, :], in_=psum[6][0:ns * 15, 0:128])
                        op("sp", "dma_start", [("ost", ob)], [("o_pool_s", l, hf, ch)], dma_key=("ost", ob),
                           out=o_pool_s[l][hf * 120:hf * 120 + ns * 15, ch * 128:(ch + 1) * 128], in_=ost[ob][0:ns * 15, :])
                    cur, nxt = u0, u1
                    scur, snxt = s0, s1
                    k = 1
                    while k < wnd:
                        op("dve", "tensor_copy", ["ub0", "ub1", "ub2"], ["ub1", "ub2"], out=nxt[:, 0:k], in_=cur[:, 0:k])
                        op("dve", "tensor_tensor", ["ub0", "ub1", "ub2"], ["ub1", "ub2"], out=nxt[:, k:UW], in0=cur[:, k:UW], in1=cur[:, 0:UW - k], op=ALU.add)
                        op("dve", "tensor_tensor", ["us0", "us1", "us2d"], ["us1", "us2d"], out=snxt[:, :, k:19], in0=scur[:, :, k:19], in1=scur[:, :, 0:19 - k], op=ALU.add)
                        if cur is u0:
                            cur, nxt = u1, u2
                            scur, snxt = s1, s2
                        else:
                            cur, nxt = nxt, cur
                            scur, snxt = snxt, scur
                        k *= 2
                    allu = ["ub0", "ub1", "ub2", "us0", "us1", "us2d"]
                    op("dve", "tensor_tensor", allu + ["invc"], ["t46"], out=t46[:, 0:16], in0=cur[:, UM:UM + 16], in1=invc[:, g, 0:16], op=ALU.mult)
                    op("dve", "tensor_tensor", allu + ["invc", "t46"], ["t46"], out=t46[:, 16:46], in0=cur[:, UH:UH + 30], in1=invc[:, g, 16:46], op=ALU.mult)
                    op("dve", "tensor_tensor", allu + ["t46"], [("dT", cc)], out=dT[:, cc, C_M:C_M + 16], in0=t46[:, 0:16], in1=u0[:, UM:UM + 16], op=ALU.subtract)
                    op("dve", "tensor_tensor", allu + ["t46"], [("dT", cc)], out=dT[:, cc, C_H:C_H + 30], in0=t46[:, 16:46], in1=u0[:, UH:UH + 30], op=ALU.subtract)
                    op("dve", "scalar_tensor_tensor", allu, [("dT", cc)], out=dT[:, cc, C_P:C_P + SPAN], in0=cur[:, UP:UP + SPAN], scalar=1.0 / wnd,
                       in1=u0[:, UP:UP + SPAN], op0=ALU.mult, op1=ALU.subtract)
                    op("dve", "scalar_tensor_tensor", allu, [("dT", cc)], out=dT[:, cc, C_S:NT].rearrange("p (s k) -> p s k", k=4),
                       in0=scur[:, :, 15:19], scalar=1.0 / wnd, in1=s0[:, :, 15:19], op0=ALU.mult, op1=ALU.subtract)
                for dd in range(2):
                    chd = g * 2 + dd
                    for ti, (t0, tw) in enumerate(TT):
                        bank = 2 + ti % 2
                        for cc in range(2):
                            op("pe", "matmul", ["poolw", ("dT", cc)], [PS(bank)], out=psum[bank][:, 0:tw],
                               lhsT=poolw[:, g * 2 + cc, dd * 128:(dd + 1) * 128], rhs=dT[:, cc, t0:t0 + tw], start=(cc == 0), stop=(cc == 1))
                        op("act", "activation", [PS(bank), "vecs"], [("poolraw", chd, ti)], out=poolraw[:, chd, t0:t0 + tw], in_=psum[bank][:, 0:tw],
                           func=AF.Copy, scale=vecs[:, 20 + chd:21 + chd])
            P.barrier()
            sq1 = [A.alloc([1, 512], BF16)[:, 0, :] for _ in range(2)]
            pn = A.alloc([2, 512], F32)
            for ti, (t0, tw) in enumerate(TT):
                for ch in range(8):
                    op("act", "activation", [("poolraw", ch, ti)], [("sq1", ch % 2)], out=sq1[ch % 2][:, 0:tw], in_=poolraw[:, ch, t0:t0 + tw], func=AF.Square)
                    op("pe", "matmul", [("sq1", ch % 2), "ones_b"], [PS(7)], out=psum[7][:, 0:tw], lhsT=ones_b, rhs=sq1[ch % 2][:, 0:tw], start=(ch == 0), stop=(ch == 7))
                op("act", "activation", [PS(7), "eps"], ["pn0"], out=pn[:, 0, 0:tw], in_=psum[7][:, 0:tw], func=AF.Sqrt, bias=eps_t[:, 0:1], scale=1.0 / POOL_WIDTH)
                op("dve", "reciprocal", ["pn0"], ["pn1"], out=pn[:, 1, 0:tw], in_=pn[:, 0, 0:tw])
                for ch in range(8):
                    op("dve", "scalar_tensor_tensor", [("poolraw", ch, ti), "vecs", "pn1"], [("poolraw", ch, ti)], out=poolraw[:, ch, t0:t0 + tw],
                       in0=poolraw[:, ch, t0:t0 + tw], scalar=vecs[:, 4 + ch:5 + ch], in1=pn[:, 1, 0:tw], op0=ALU.mult, op1=ALU.mult)
            wout_half(l, 0, poolraw, "poolraw", wt)
            P.barrier()
            A.reset(m_e)

            heads(l, qlat, vecs, gqk, b192)
            P.barrier()
            A.reset(m_mix)

        def wout_half(l, half, cat, catname, wt):
            wo_v = w_out_d[l].rearrange("(c p) f -> p c f", p=128)
            for i in range(NCH):
                wb = i % 2
                ld(wt[wb][:, 0:8, :], wo_v[:, half * 8:half * 8 + 8, i * 128:(i + 1) * 128], 1024, ("wt", wb), f=128)
                for ti, (t0, tw) in enumerate(TT):
                    bank = 2 + (i * len(TT) + ti) % 2
                    for c in range(8):
                        op("pe", "matmul", [("wt", wb), (catname, c, ti)], [PS(bank)], out=psum[bank][:, 0:tw], lhsT=wt[wb][:, c, :],
                           rhs=cat[:, c, t0:t0 + tw], start=(c == 0), stop=(c == 7))
                    op("dve", "tensor_tensor", [PS(bank)] + xT_keys(i, t0, tw), xT_keys(i, t0, tw), out=xT[:, i, t0:t0 + tw],
                       in0=psum[bank][:, 0:tw], in1=xT[:, i, t0:t0 + tw], op=ALU.add)

        def heads(l, qlat, vecs, gqk, b192):
            XA = Arena(arena_t, xn_words, base=xn_off)
            SA = Arena(arena_t, NSTG * 2048, base=st_off)

            def xalloc(shape, dt):
                return XA.alloc(shape, dt) if XA.room(shape, dt) else A.alloc(shape, dt)

            attn = XA.alloc([8, NT], BF16)
            wqb = xalloc([4, 1536], BF16)
            wqrot = xalloc([4, 512], BF16)
            for c in range(4):
                ld(wqb[:, c, :], w_qb_d[l][c * 128:(c + 1) * 128, :], 1536, ("wqb", c))
            for c in range(4):
                src = wqb[:, c, :].rearrange("p (h d) -> p h d", d=192)
                dst = wqrot[:, c, :].rearrange("p (h j) -> p h j", j=64)
                op("dve", "tensor_scalar", [("wqb", c)], [("wqrot", c)], out=dst[:, :, 0:32], in0=src[:, :, 160:192], scalar1=-1.0, scalar2=None, op0=ALU.mult)
                op("dve", "tensor_copy", [("wqb", c)], [("wqrot", c)], out=dst[:, :, 32:64], in_=src[:, :, 128:160])
            wnope = A.alloc([2, 1024], BF16)
            wv = A.alloc([2, 1024], BF16)
            wnT = A.alloc([8, 256], BF16)
            for cc in range(2):
                si = next_stage()
                op("sp", "dma_start", [], [("stage", si)], dma_key=("stage", si), out=stage[si], in_=w_kvb_d[l][cc * 128:(cc + 1) * 128, :])
                v3 = stage[si].rearrange("p (h d) -> p h d", d=256)
                op("dve", "tensor_copy", [("stage", si)], [("wnope", cc)], out=wnope[:, cc, :].rearrange("p (h d) -> p h d", d=128), in_=v3[:, :, 0:128])
                op("pool", "tensor_copy", [("stage", si)], [("wv", cc)], out=wv[:, cc, :].rearrange("p (h d) -> p h d", d=128), in_=v3[:, :, 128:256])
            psb = [psum[i][:, :].bitcast(BF16) for i in range(8)]
            for h in range(N_HEADS):
                for cc in range(2):
                    op("pe", "transpose", [("wnope", cc), "ident_b"], [PS(0)], out=psb[0][:, 0:128], in_=wnope[:, cc, h * 128:(h + 1) * 128], identity=ident_b)
                    op("dve", "tensor_copy", [PS(0)], [("wnT", h)], out=wnT[:, h, cc * 128:(cc + 1) * 128], in_=psb[0][:, 0:128])
            qabS = A.alloc([2, NSEQ * 32], BF16)
            qrS = A.alloc([1, NSEQ * 32], BF16)[:, 0, :]
            sqk2 = [A.alloc([1, 1024], BF16)[:, 0, :] for _ in range(2)]
            j642 = [A.alloc([1, 64], F32)[:, 0, :] for _ in range(2)]
            s82 = [A.alloc([4, 8], F32) for _ in range(2)]
            P.barrier()
            m_keys = A.mark()

            def prep(n, srcf, skey, cb, cT, kT, rstd, okeys, par, tb, kn0):
                sqk_, j64_, s8_ = sqk2[par], j642[par], s82[par]
                op("act", "activation", [skey], [okeys[0]], out=cb[0:n, :], in_=srcf[0:n, :], func=AF.Copy)
                for cc in range(2):
                    op("pe", "transpose", [okeys[0], "ident_b"], [PS(tb)], out=psb[tb][:, cc * 128:cc * 128 + n], in_=cb[0:n, cc * 128:(cc + 1) * 128],
                       identity=ident_b[0:n, 0:n])
                op("pe", "transpose", [okeys[0], "ident_b"], [PS(tb)], out=psb[tb][0:64, 256:256 + n], in_=cb[0:n, 256:320], identity=ident_b[0:n, 0:n])
                op("dve", "tensor_copy", [PS(tb)], [okeys[1]], out=cT[:, :, 0:n], in_=psb[tb][:, 0:256].rearrange("p (c k) -> p c k", k=128)[:, :, 0:n])
                op("dve", "tensor_copy", [PS(tb)], [okeys[2]], out=kT[0:64, 0:n], in_=psb[tb][0:64, 256:256 + n])
                for half in range(2):
                    for cc in range(2):
                        op("pe", "matmul", [okeys[1], ("wnope", cc)], [PS(kn0 + half)], out=psum[kn0 + half][0:n, 0:512], lhsT=cT[:, cc, 0:n],
                           rhs=wnope[:, cc, half * 512:(half + 1) * 512], start=(cc == 0), stop=(cc == 1))
                    op("act", "activation", [PS(kn0 + half)], [("sqk", par, half)], out=sqk_[0:n, half * 512:(half + 1) * 512], in_=psum[kn0 + half][0:n, 0:512], func=AF.Square)
                    op("dve", "tensor_reduce", [("sqk", par, half)], [("s8a", par, half)], out=s8_[0:n, 0, half * 4:(half + 1) * 4],
                       in_=sqk_[0:n, half * 512:(half + 1) * 512].rearrange("p (h d) -> p h d", d=128), axis=AX.X, op=ALU.add)
                op("act", "activation", [skey], [("j64", par)], out=j64_[0:n, :], in_=srcf[0:n, 256:320], func=AF.Square)
                op("dve", "tensor_reduce", [("j64", par)], [("s8b", par)], out=s8_[0:n, 1, 0:1], in_=j64_[0:n, :], axis=AX.X, op=ALU.add)
                op("dve", "tensor_scalar", [("s8b", par), "b192"], [("s8bb", par)], out=s8_[0:n, 1, 1:2], in0=s8_[0:n, 1, 0:1], scalar1=b192[0:n, 0:1], scalar2=None, op0=ALU.add)
                op("act", "activation", [("s8a", par, 0), ("s8a", par, 1), ("s8bb", par)], [("s8d", par)], out=s8_[0:n, 3, :], in_=s8_[0:n, 0, :], func=AF.Ln,
                   bias=s8_[0:n, 1, 1:2], scale=1.0)
                op("act", "activation", [("s8d", par)], [okeys[3]], out=rstd[0:n, :], in_=s8_[0:n, 3, :], func=AF.Exp, scale=-0.5)

            kpos = A.alloc([1, NKB], F32)[:, 0, :]
            ldf(kpos, kpos_d, "kpos")
            cT_all = A.alloc([2, NKB * 128], BF16)
            kT_all = xalloc([1, NKB * 128], BF16)[:, 0, :]
            rstd_all = xalloc([NKB, 8], F32)
            kst = [xalloc([1, 320], F32)[:, 0, :] for _ in range(2)]
            pck = [A.alloc([1, 320], BF16)[:, 0, :] for _ in range(2)]
            for kb, (q, r0, n) in enumerate(KBS):
                b = kb % 2
                op("sp", "dma_start", [("ccout", l, q)], [("kst", b)], dma_key=("kst", b), out=kst[b][0:n, :], in_=cc_out[l][q][r0:r0 + n, :])
                prep(n, kst[b], ("kst", b), pck[b], cT_all[:, :, kb * 128:(kb + 1) * 128], kT_all[:, kb * 128:(kb + 1) * 128], rstd_all[:, kb, :],
                     [("pck", b), ("cT", kb), ("kT", kb), ("rstd", kb)], b, b * 3, b * 3 + 1)
                op("sp", "dma_start", [("pck", b)], [("ctokd", l, kb)], dma_key=("pck", b), out=ctok_d[l][kb * 128:kb * 128 + n, :], in_=pck[b][0:n, 0:256])
            tq = SA.alloc([2, 512], F32)
            rr = SA.alloc([2, 512], F32)
            sqn = SA.alloc([2, 512], BF16)
            qne = SA.alloc([1, 512], BF16)[:, 0, :]
            qre = SA.alloc([1, 512], BF16)[:, 0, :]
            qab = SA.alloc([2, 512], BF16)
            olat = SA.alloc([2, 512], BF16)
            pT = [SA.alloc([1, 512], BF16)[:, 0, :] for _ in range(2)]
            mk = [SA.alloc([1, 512], BF16)[:, 0, :] for _ in range(2)]
            pTm = [SA.alloc([1, 512], BF16)[:, 0, :] for _ in range(2)]
            rden = SA.alloc([1, 512], F32)[:, 0, :]
            NCB = 4
            cbuf = [A.alloc([1, 256], BF16)[:, 0, :] for _ in range(NCB)]
            cosb = A.alloc([1, 512], F32)[:, 0, :]
            sinb = A.alloc([1, 512], F32)[:, 0, :]
            qposb = A.alloc([1, 512], F32)[:, 0, :]
            cbi = [0]
            for ti, (t0, tw) in enumerate(TT):
                qw = min(t0 + tw, NQ) - t0
                op("sp", "dma_start", [], ["cosb"], dma_key="cosb", out=cosb[0:64, 0:tw], in_=cosT_d[:, t0:t0 + tw])
                op("sp", "dma_start", [], ["sinb"], dma_key="sinb", out=sinb[0:64, 0:tw], in_=sinT_d[:, t0:t0 + tw])
                if qw > 0:
                    op("sp", "dma_start", [], ["qposb"], dma_key="qposb", out=qposb[:, 0:qw], in_=qpos_d[:, t0:t0 + qw])
                for h in range(N_HEADS):
                    for c in range(4):
                        op("pe", "matmul", [("wqb", c), ("qlat", c, ti)], [PS(0)], out=psum[0][:, 0:tw], lhsT=wqb[:, c, h * 192:h * 192 + 128],
                           rhs=qlat[:, c, t0:t0 + tw], start=(c == 0), stop=(c == 3))
                    for c in range(4):
                        op("pe", "matmul", [("wqb", c), ("qlat", c, ti)], [PS(1)], out=psum[1][0:64, 0:tw], lhsT=wqb[:, c, h * 192 + 128:h * 192 + 192],
                           rhs=qlat[:, c, t0:t0 + tw], start=(c == 0), stop=(c == 3))
                    for c in range(4):
                        op("pe", "matmul", [("wqrot", c), ("qlat", c, ti)], [PS(2)], out=psum[2][0:64, 0:tw], lhsT=wqrot[:, c, h * 64:(h + 1) * 64],
                           rhs=qlat[:, c, t0:t0 + tw], start=(c == 0), stop=(c == 3))
                    op("dve", "tensor_tensor", [PS(1), "cosb"], ["tq0"], out=tq[0:64, 0, 0:tw], in0=psum[1][0:64, 0:tw], in1=cosb[0:64, 0:tw], op=ALU.mult)
                    op("dve", "tensor_tensor", [PS(2), "sinb"], ["tq1"], out=tq[0:64, 1, 0:tw], in0=psum[2][0:64, 0:tw], in1=sinb[0:64, 0:tw], op=ALU.mult)
                    op("dve", "tensor_tensor", ["tq0", "tq1"], ["tq0"], out=tq[0:64, 0, 0:tw], in0=tq[0:64, 0, 0:tw], in1=tq[0:64, 1, 0:tw], op=ALU.add)
                    op("act", "activation", [PS(0)], ["sqn0"], out=sqn[:, 0, 0:tw], in_=psum[0][:, 0:tw], func=AF.Square)
                    op("act", "activation", ["tq0"], ["sqn1"], out=sqn[0:64, 1, 0:tw], in_=tq[0:64, 0, 0:tw], func=AF.Square)
                    op("pe", "matmul", ["sqn0", "ones_b"], [PS(3)], out=psum[3][:, 0:tw], lhsT=ones_b, rhs=sqn[:, 0, 0:tw], start=True, stop=False)
                    op("pe", "matmul", ["sqn1", "ones_b"], [PS(3)], out=psum[3][:, 0:tw], lhsT=ones_b[0:64, :], rhs=sqn[0:64, 1, 0:tw], start=False, stop=True)
                    op("act", "activation", [PS(3), "eps"], ["rr0"], out=rr[:, 0, 0:tw], in_=psum[3][:, 0:tw], func=AF.Sqrt, bias=eps_t[:, 0:1], scale=1.0 / QK_DIM)
                    op("dve", "reciprocal", ["rr0"], ["rr1"], out=rr[:, 1, 0:tw], in_=rr[:, 0, 0:tw])
                    op("dve", "scalar_tensor_tensor", [PS(0), "gqk", "rr1"], ["qne"], out=qne[:, 0:tw], in0=psum[0][:, 0:tw], scalar=gqk[:, 0:1], in1=rr[:, 1, 0:tw],
                       op0=ALU.mult, op1=ALU.mult)
                    op("dve", "scalar_tensor_tensor", ["tq0", "gqk", "rr1"], ["qre"], out=qre[0:64, 0:tw], in0=tq[0:64, 0, 0:tw], scalar=gqk[0:64, 1:2], in1=rr[0:64, 1, 0:tw],
                       op0=ALU.mult, op1=ALU.mult)
                    for cc in range(2):
                        op("pe", "matmul", ["qne", ("wnT", h)], [PS(4 + cc)], out=psum[4 + cc][:, 0:tw], lhsT=wnT[:, h, cc * 128:(cc + 1) * 128], rhs=qne[:, 0:tw],
                           start=True, stop=True)
                        op("act", "activation", [PS(4 + cc)], [("qab", cc)], out=qab[:, cc, 0:tw], in_=psum[4 + cc][:, 0:tw], func=AF.Copy)
                    for nm, so, src, n in seg_iter(t0, tw):
                        if nm == "S":
                            s_a, s_b = so // 4, (so + n) // 4
                            for cc in range(2):
                                op("pool", "tensor_copy", [("qab", cc)], [("qabS", h)],
                                   out=qabS[:, cc, :].rearrange("p (s h t) -> p s h t", h=8, t=4)[:, s_a:s_b, h, :],
                                   in_=qab[:, cc, src:src + n].rearrange("p (s t) -> p s t", t=4))
                            op("pool", "tensor_copy", ["qre"], [("qrS", h)], out=qrS[0:64, :].rearrange("p (s h t) -> p s h t", h=8, t=4)[:, s_a:s_b, h, :],
                               in_=qre[0:64, src:src + n].rearrange("p (s t) -> p s t", t=4))
                    if qw <= 0:
                        continue
                    for kb, (q_, r0, n) in enumerate(KBS):
                        sb = kb % 2
                        ks = slice(kb * 128, kb * 128 + n)
                        cb = cbi[0] % NCB
                        cbi[0] += 1
                        op("sp", "dma_start", [("ctokd", l, kb)], [("cbuf", cb)], dma_key=("cbuf", cb), out=cbuf[cb][0:n, :], in_=ctok_d[l][kb * 128:kb * 128 + n, :])
                        op("pe", "matmul", [("cT", kb), ("qab", 0)], [PS(sb)], out=psum[sb][0:n, 0:qw], lhsT=cT_all[:, 0, ks], rhs=qab[:, 0, 0:qw], start=True, stop=False)
                        op("pe", "matmul", [("cT", kb), ("qab", 1)], [PS(sb)], out=psum[sb][0:n, 0:qw], lhsT=cT_all[:, 1, ks], rhs=qab[:, 1, 0:qw], start=False, stop=False)
                        op("pe", "matmul", [("kT", kb), "qre"], [PS(sb)], out=psum[sb][0:n, 0:qw], lhsT=kT_all[0:64, ks], rhs=qre[0:64, 0:qw], start=False, stop=True)
                        op("act", "activation", [PS(sb), ("rstd", kb)], [("pT", sb)], out=pT[sb][0:n, 0:qw], in_=psum[sb][0:n, 0:qw], func=AF.Exp,
                           scale=rstd_all[0:n, kb, h:h + 1])
                        op("dve", "tensor_scalar", ["qposb", "kpos"], [("mk", sb)], out=mk[sb][0:n, 0:qw], in0=qposb[0:n, 0:qw], scalar1=kpos[0:n, kb:kb + 1],
                           scalar2=None, op0=ALU.is_ge)
                        op("pool", "tensor_tensor", [("pT", sb), ("mk", sb)], [("pTm", sb)], out=pTm[sb][0:n, 0:qw], in0=pT[sb][0:n, 0:qw], in1=mk[sb][0:n, 0:qw], op=ALU.mult)
                        first, last = (kb == 0), (kb == NKB - 1)
                        for cc in range(2):
                            op("pe", "matmul", [("cbuf", cb), ("pTm", sb)], [PS(2 + cc)], out=psum[2 + cc][:, 0:qw], lhsT=cbuf[cb][0:n, cc * 128:(cc + 1) * 128],
                               rhs=pTm[sb][0:n, 0:qw], start=first, stop=last)
                        op("pe", "matmul", ["ones_b", ("pTm", sb)], [PS(6)], out=psum[6][:, 0:qw], lhsT=ones_b[0:n, :], rhs=pTm[sb][0:n, 0:qw], start=first, stop=last)
                    op("dve", "reciprocal", [PS(6)], ["rden"], out=rden[:, 0:qw], in_=psum[6][:, 0:qw])
                    for cc in range(2):
                        op("dve", "tensor_tensor", [PS(2 + cc), "rden"], [("olat", cc)], out=olat[:, cc, 0:qw], in0=psum[2 + cc][:, 0:qw], in1=rden[:, 0:qw], op=ALU.mult)
                    for cc in range(2):
                        op("pe", "matmul", [("olat", cc), ("wv", cc)], [PS(7)], out=psum[7][:, 0:qw], lhsT=wv[:, cc, h * 128:(h + 1) * 128], rhs=olat[:, cc, 0:qw],
                           start=(cc == 0), stop=(cc == 1))
                    op("act", "activation", [PS(7)], [("attn", h, ti)], out=attn[:, h, t0:t0 + qw], in_=psum[7][:, 0:qw], func=AF.Copy)
            P.barrier()
            A.reset(m_keys)
            ptb = A.alloc([1, NSEQ * NPAGES], I32)[0:1, 0, :]
            ldf(ptb, pt_d, "ptb")
            mknew = A.alloc([1, 32], BF16)[:, 0, :]
            mknew_f = A.alloc([1, 32], F32)[:, 0, :]
            ldf(mknew_f, mknew_d, "mknew_f")
            op("dve", "tensor_copy", ["mknew_f"], ["mknew"], out=mknew, in_=mknew_f)
            pst = [A.alloc([1, 320], F32)[:, 0, :] for _ in range(2)]
            pctok = [A.alloc([1, 320], BF16)[:, 0, :] for _ in range(2)]
            pcT = [A.alloc([2, 128], BF16) for _ in range(2)]
            pkT = [A.alloc([1, 128], BF16)[:, 0, :] for _ in range(2)]
            prs = [A.alloc([1, 8], F32)[:, 0, :] for _ in range(2)]
            sc2 = [A.alloc([1, 32], F32)[:, 0, :] for _ in range(2)]
            pTs = [A.alloc([1, 32], BF16)[:, 0, :] for _ in range(2)]
            rdS = A.alloc([1, 32], F32)[:, 0, :]
            olS = A.alloc([2, 32], BF16)
            regs = REGS
            def gather(e, s, pg, b, which):
                if "pg" not in regs:
                    regs["pg"] = e.alloc_register("r_pg")
                    regs["c"] = e.alloc_register("r_c")
                e.reg_load(regs["pg"], ptb[0:1, s * NPAGES + pg:s * NPAGES + pg + 1])
                if which == 0:
                    e.reg_mul(regs["c"], regs["pg"], 128 * 256 * 4)
                    rc = bass.RuntimeValue.of_register_unchecked(regs["c"])
                    return e.dma_start(out=pst[b][:, 0:256].bitcast(U8), in_=cckv_d[l].bitcast(U8)[bass.ds(rc, 128 * 1024)].rearrange("(p d) -> p d", d=1024))
                e.reg_mul(regs["c"], regs["pg"], 128 * 64 * 4)
                rc = bass.RuntimeValue.of_register_unchecked(regs["c"])
                return e.dma_start(out=pst[b][:, 256:320].bitcast(U8), in_=ckpe_d[l].bitcast(U8)[bass.ds(rc, 128 * 256)].rearrange("(p d) -> p d", d=256))

            for s in range(NSEQ):
                for pg in range(NPAGES + 1):
                    b = (s * (NPAGES + 1) + pg) % 2
                    if pg < NPAGES:
                        n = 128
                        P.op("sp", (lambda e, s=s, pg=pg, b=b: gather(e, s, pg, b, 0)), ["ptb"], [("pst", b)], dma_key=("pst", b))
                        P.op("sp", (lambda e, s=s, pg=pg, b=b: gather(e, s, pg, b, 1)), ["ptb"], [("pst", b)], dma_key=("pst", b))
                    else:
                        n = DEC_SEQ
                        op("sp", "dma_start", [("kvS", l)], [("pst", b)], dma_key=("pst", b), out=pst[b][0:n, :], in_=kvS[l][s * 4:(s + 1) * 4, :])
                    prep(n, pst[b], ("pst", b), pctok[b], pcT[b], pkT[b], prs[b], [("pctok", b), ("pcT", b), ("pkT", b), ("prs", b)], b, (0 if b == 0 else 7), 1)
                    sc = sc2[b]
                    so = b * 32
                    op("pe", "matmul", [("pcT", b)] + [("qabS", h) for h in range(8)], [PS(3)], out=psum[3][0:n, so:so + 32], lhsT=pcT[b][:, 0, 0:n],
                       rhs=qabS[:, 0, s * 32:(s + 1) * 32], start=True, stop=False)
                    op("pe", "matmul", [("pcT", b)], [PS(3)], out=psum[3][0:n, so:so + 32], lhsT=pcT[b][:, 1, 0:n], rhs=qabS[:, 1, s * 32:(s + 1) * 32], start=False, stop=False)
                    op("pe", "matmul", [("pkT", b)] + [("qrS", h) for h in range(8)], [PS(3)], out=psum[3][0:n, so:so + 32], lhsT=pkT[b][0:64, 0:n],
                       rhs=qrS[0:64, s * 32:(s + 1) * 32], start=False, stop=True)
                    op("dve", "tensor_tensor", [PS(3), ("prs", b)], [("sc", b)], out=sc[0:n, :].rearrange("p (h t) -> p h t", t=4),
                       in0=psum[3][0:n, so:so + 32].rearrange("p (h t) -> p h t", t=4), in1=prs[b][0:n, :].unsqueeze(2).broadcast_to([n, 8, 4]), op=ALU.mult)
                    op("act", "activation", [("sc", b)], [("pTs", b)], out=pTs[b][0:n, :], in_=sc[0:n, :], func=AF.Exp)
                    if pg == NPAGES:
                        op("dve", "tensor_tensor", [("pTs", b), "mknew"], [("pTs", b)], out=pTs[b][0:n, :], in0=pTs[b][0:n, :], in1=mknew[0:n, :], op=ALU.mult)
                    first, last = (pg == 0), (pg == NPAGES)
                    for cc in range(2):
                        op("pe", "matmul", [("pctok", b), ("pTs", b)], [PS(4 + cc)], out=psum[4 + cc][:, 0:32], lhsT=pctok[b][0:n, cc * 128:(cc + 1) * 128],
                           rhs=pTs[b][0:n, :], start=first, stop=last)
                    op("pe", "matmul", ["ones_b", ("pTs", b)], [PS(6)], out=psum[6][:, 0:32], lhsT=ones_b[0:n, :], rhs=pTs[b][0:n, :], start=first, stop=last)
                op("dve", "reciprocal", [PS(6)], ["rdS"], out=rdS, in_=psum[6][:, 0:32])
                for cc in range(2):
                    op("dve", "tensor_tensor", [PS(4 + cc), "rdS"], [("olS", cc)], out=olS[:, cc, :], in0=psum[4 + cc][:, 0:32], in1=rdS, op=ALU.mult)
                for h in range(N_HEADS):
                    for cc in range(2):
                        op("pe", "matmul", [("olS", cc), ("wv", cc)], [PS(3)], out=psum[3][:, 128 + h * 4:128 + (h + 1) * 4], lhsT=wv[:, cc, h * 128:(h + 1) * 128],
                           rhs=olS[:, cc, h * 4:(h + 1) * 4], start=(cc == 0), stop=(cc == 1))
                tis = [ti for ti, (a, bb) in enumerate(TT) if a < C_S + (s + 1) * 4 and C_S + s * 4 < a + bb]
                op("act", "activation", [PS(3)], [("attn", h, ti) for h in range(8) for ti in tis], out=attn[:, :, C_S + s * 4:C_S + (s + 1) * 4],
                   in_=psum[3][:, 128:160].rearrange("p (h t) -> p h t", t=4), func=AF.Copy)
            P.barrier()
            A.reset(m_keys)
            sq1 = [A.alloc([1, 512], BF16)[:, 0, :] for _ in range(2)]
            nr = A.alloc([2, 512], F32)
            for ti, (t0, tw) in enumerate(TT):
                for ch in range(8):
                    op("act", "activation", [("attn", ch, ti)], [("sq1", ch % 2)], out=sq1[ch % 2][:, 0:tw], in_=attn[:, ch, t0:t0 + tw], func=AF.Square)
                    op("pe", "matmul", [("sq1", ch % 2), "ones_b"], [PS(7)], out=psum[7][:, 0:tw], lhsT=ones_b, rhs=sq1[ch % 2][:, 0:tw], start=(ch == 0), stop=(ch == 7))
                op("act", "activation", [PS(7), "eps"], ["nr0"], out=nr[:, 0, 0:tw], in_=psum[7][:, 0:tw], func=AF.Sqrt, bias=eps_t[:, 0:1], scale=1.0 / 1024)
                op("dve", "reciprocal", ["nr0"], ["nr1"], out=nr[:, 1, 0:tw], in_=nr[:, 0, 0:tw])
                for ch in range(8):
                    op("dve", "scalar_tensor_tensor", [("attn", ch, ti), "vecs", "nr1"], [("attn", ch, ti)], out=attn[:, ch, t0:t0 + tw],
                       in0=attn[:, ch, t0:t0 + tw], scalar=vecs[:, 12 + ch:13 + ch], in1=nr[:, 1, 0:tw], op0=ALU.mult, op1=ALU.mult)
            wt2 = [A.alloc([8, 128], BF16) for _ in range(2)]
            wout_half(l, 1, attn, "attn", wt2)
            hv = A.alloc([NCH, NHALO], F32)
            ldf(hv, hv_d.rearrange("p (c n) -> p c n", n=NHALO), "hv")
            hk = [k for c in range(NCH) for k in xT_keys(c, C_H, NHALO)]
            op("dve", "tensor_tensor", hk + ["hv"], hk, out=xT[:, :, C_H:C_H + NHALO], in0=xT[:, :, C_H:C_H + NHALO], in1=hv, op=ALU.mult)

        for l in range(DEPTH):
            ffn(l, "ffn1")
            if stop_after != "nomix":
                mixer(l)
            ffn(l, "ffn2")

        for bi, (t0, tw) in enumerate(TB):
            si = next_stage()
            for c4 in range(NCH // 4):
                bank = c4 % 2
                for cc in range(4):
                    c = c4 * 4 + cc
                    P.op("pe", (lambda e, bank=bank, cc=cc, c=c, t0=t0, tw=tw: e.transpose(
                        out=psum[bank][0:tw, cc * 128:(cc + 1) * 128], in_=xT[:, c, t0:t0 + tw], identity=ident_f)),
                         xT_keys(c, t0, tw) + ["ident_f"], [PS(bank)])
                if c4 % 2 == 0:
                    P.op("dve", (lambda e, bank=bank, si=si, c4=c4, tw=tw: e.tensor_copy(
                        out=stage[si][0:tw, c4 * 512:(c4 + 1) * 512], in_=psum[bank][0:tw, :])), [PS(bank)], [("stage", si)])
                else:
                    P.op("act", (lambda e, bank=bank, si=si, c4=c4, tw=tw: e.activation(
                        out=stage[si][0:tw, c4 * 512:(c4 + 1) * 512], in_=psum[bank][0:tw, :], func=AF.Copy)), [PS(bank)], [("stage", si)])
            P.op("sp", (lambda e, si=si, t0=t0, tw=tw: e.dma_start(out=y_tok[t0:t0 + tw, :], in_=stage[si][0:tw, :])),
                 [("stage", si)], [("y", bi)], dma_key=("stage", si))
        P.barrier()
        nsem = P.emit(nc, lambda name: es.enter_context(nc.semaphore(name)))
    return nc, dict(NT=NT, nsem=nsem, peak=A.peak, ninstr={e: len(P.per_eng[e]) for e in ENGS})


_CACHE = {}


def _rope_tables(pos):
    inv = (np.float32(ROPE_THETA) ** (-np.arange(32, dtype=np.float32) / np.float32(32))).astype(np.float32)
    ang = pos.astype(np.float32)[:, None] * inv[None, :]
    return np.cos(ang).astype(np.float32), np.sin(ang).astype(np.float32)


def run_cfg(cfg, inputs):
    SPAN, NSEQ, NPAGES, DFF, NPOOLPG, DEPTH = (cfg[k] for k in ("SPAN", "NSEQ", "NPAGES", "DFF", "NPOOLPG", "DEPTH"))
    key = tuple(sorted(cfg.items()))
    if key not in _CACHE:
        _CACHE[key] = build(cfg)
    nc, info = _CACHE[key]
    NS = NSEQ * DEC_SEQ
    NR = N_META + SPAN
    C_P, C_H = N_META, N_META + SPAN
    C_S = C_H + NHALO
    NT = C_S + NS
    NQ = C_S
    PAST = NPAGES * PAGE
    f32 = np.float32
    g = {k: np.asarray(v) for k, v in inputs.items()}
    L = DEPTH

    def featT(v, nch):
        return np.ascontiguousarray(v.reshape(L, nch, 128).transpose(0, 2, 1)).astype(f32)

    shared = {"ident": np.eye(128, dtype=f32)}
    for nm in ("ffn1", "ffn2"):
        for s in ("_w_gate", "_w_up", "_w_down"):
            shared[nm + s] = g[nm + s]
        shared[nm + "_normT"] = featT(g[nm + "_norm"], NCH)
    shared["w_in"] = g["w_in"]
    shared["w_q_b"] = g["w_q_b"]
    shared["w_kv_b"] = g["w_kv_b"]
    shared["pool_w"] = g["pool_w"]
    shared["w_out"] = g["w_out"]
    shared["mix_normT"] = featT(g["mix_norm"], NCH)
    vecs = np.zeros((L, 128, 32), f32)
    vecs[:, :, 0:4] = featT(g["q_a_norm"], 4)
    vecs[:, :, 4:12] = featT(g["pool_out_norm"], 8)
    vecs[:, :, 12:20] = featT(g["attn_out_norm"], 8)
    vecs[:, :, 20:28] = featT(g["pool_scale"], 8)
    vecs[:, :, 28] = g["q_norm_nope"]
    vecs[:, :, 29] = g["k_norm_nope"]
    vecs[:, 0:64, 30] = np.concatenate([g["q_norm_rope"], g["q_norm_rope"]], -1)
    vecs[:, 0:64, 31] = np.concatenate([g["k_norm_rope"], g["k_norm_rope"]], -1)
    shared["vecs"] = vecs
    shared["kvnorm_b"] = np.ascontiguousarray(np.broadcast_to(g["kv_a_norm"][:, None, :], (L, 128, 256))).astype(f32)
    mk = np.zeros((128, 32), f32)
    for k in range(4):
        for h in range(8):
            for t in range(4):
                mk[k, h * 4 + t] = 1.0 if t >= k else 0.0
    shared["mknew"] = mk
    NKB_ = 1 + 4 * (SPAN // 128)
    kpos = np.zeros((128, NKB_), f32)
    kpos[:, 0] = np.arange(128)
    kpos[16:, 0] = 1e9
    for kb in range(1, NKB_):
        r, i = divmod(kb - 1, SPAN // 128)
        kpos[:, kb] = 16 + r * SPAN + i * 128 + np.arange(128)
    shared["kpos_t"] = kpos
    for ll in range(L):
        shared["cache_ckv%d" % ll] = g["cache_ckv"][ll].reshape(-1, 256)
        shared["cache_kpe%d" % ll] = g["cache_kpe"][ll].reshape(-1, 64)

    in_maps = []
    for core in range(8):
        b, j = divmod(core, 4)
        m = dict(shared)
        xp = g["x_prompt"][b]
        meta = g["meta_tokens"]
        if j == 0:
            halo = np.concatenate([np.zeros((14, D_MODEL), f32), meta], 0)
            hpos = np.concatenate([np.zeros(14), np.arange(16)])
            hval = np.concatenate([np.zeros(14), np.ones(16)])
        else:
            halo = xp[j * SPAN - NHALO:j * SPAN]
            hpos = 16 + j * SPAN - NHALO + np.arange(NHALO)
            hval = np.ones(NHALO)
        xs = g["x_sample"][core * NSEQ:(core + 1) * NSEQ].reshape(NS, D_MODEL)
        m["x_tok"] = np.ascontiguousarray(np.concatenate([meta, xp[j * SPAN:(j + 1) * SPAN], halo, xs], 0)).astype(f32)
        pos = np.concatenate([np.arange(16), 16 + j * SPAN + np.arange(SPAN), hpos, np.tile(PAST + np.arange(DEC_SEQ), NSEQ)]).astype(f32)
        m["qpos_b"] = np.ascontiguousarray(np.broadcast_to(pos[None, :NQ], (128, NQ))).astype(f32)
        m["hvtab"] = np.ascontiguousarray(np.broadcast_to(np.tile(hval, NCH)[None, :], (128, NCH * NHALO))).astype(f32)
        pmh = np.concatenate([np.arange(16), hpos])
        invc = np.stack([1.0 / np.minimum(pmh + 1, wd) for wd in POOL_WINDOWS], 0).reshape(-1)
        m["invcnt"] = np.ascontiguousarray(np.broadcast_to(invc[None, :], (128, 4 * 46))).astype(f32)
        c, s = _rope_tables(pos)
        m["cos_tok"], m["sin_tok"] = c, s
        m["cosT"] = np.ascontiguousarray(np.concatenate([c.T, c.T], 0))
        m["sinT"] = np.ascontiguousarray(np.concatenate([s.T, s.T], 0))
        m["state_pool"] = np.ascontiguousarray(g["state_pool"][:, core * NSEQ:(core + 1) * NSEQ].reshape(L, NSEQ * 15, POOL_WIDTH))
        m["page_table"] = np.ascontiguousarray(g["page_table"][core * NSEQ:(core + 1) * NSEQ].reshape(1, NSEQ * NPAGES)).astype(np.int32)
        in_maps.append(m)
    res = run_bass_kernel_spmd(nc, in_maps, core_ids=list(range(8)))
    R = res.results
    B = 2
    T = N_META + 4 * SPAN
    y_prompt = np.zeros((B, 4 * SPAN, D_MODEL), f32)
    y_sample = np.zeros((8 * NSEQ, DEC_SEQ, D_MODEL), f32)
    ckv_p = np.zeros((L, B, T, 256), f32)
    kpe_p = np.zeros((L, B, T, 64), f32)
    pool_p = np.zeros((L, B, 15, POOL_WIDTH), f32)
    ckv_s = np.zeros((L, 8 * NSEQ, DEC_SEQ, 256), f32)
    kpe_s = np.zeros((L, 8 * NSEQ, DEC_SEQ, 64), f32)
    pool_s = np.zeros((L, 8 * NSEQ, 15, POOL_WIDTH), f32)
    for core in range(8):
        b, j = divmod(core, 4)
        r = R[core]
        y_prompt[b, j * SPAN:(j + 1) * SPAN] = r["y_tok"][C_P:C_P + SPAN]
        y_sample[core * NSEQ:(core + 1) * NSEQ] = r["y_tok"][C_S:NT].reshape(NSEQ, DEC_SEQ, D_MODEL)
        ckv_p[:, b, 16 + j * SPAN:16 + (j + 1) * SPAN] = r["o_ckv_p"][:, 16:]
        kpe_p[:, b, 16 + j * SPAN:16 + (j + 1) * SPAN] = r["o_kpe_p"][:, 16:]
        if j == 0:
            ckv_p[:, b, 0:16] = r["o_ckv_p"][:, 0:16]
            kpe_p[:, b, 0:16] = r["o_kpe_p"][:, 0:16]
        if j == 3:
            pool_p[:, b] = r["o_pool_p"]
        ckv_s[:, core * NSEQ:(core + 1) * NSEQ] = r["o_ckv_s"].reshape(L, NSEQ, DEC_SEQ, 256)
        kpe_s[:, core * NSEQ:(core + 1) * NSEQ] = r["o_kpe_s"].reshape(L, NSEQ, DEC_SEQ, 64)
        pool_s[:, core * NSEQ:(core + 1) * NSEQ] = r["o_pool_s"].reshape(L, NSEQ, 15, POOL_WIDTH)
    return (y_prompt, y_sample, ckv_p, kpe_p, pool_p, ckv_s, kpe_s, pool_s)


def kernel(**inputs):
    return run_cfg(FULL_CFG, inputs)
```
